# Optimizing a Trainium2 kernel written in Bass

```python
import math
import jax
import jax.numpy as jnp
from jax import lax
import numpy as np

D_MODEL = 2048
BATCH = 4
SEQ = 2048
DEPTH = 4

HEAD_DIM = 128
NSA_HEADS = 8
NSA_KV_HEADS = 2
CMP_LEN = 32
CMP_STRIDE = 16
CMP_HIDDEN = 2 * HEAD_DIM
SEL_BLOCK = 64
SEL_TOP_N = 16
NSA_WINDOW = 512
NSA_Q_CHUNK = 64
SWA_HEADS = 8
SWA_KV_HEADS = 2
SWA_WINDOW = 128
MOBA_HEADS = 8
MOBA_KV_HEADS = 2
MOBA_BLOCK = 256
MOBA_TOP_K = 3
MOBA_Q_CHUNK = 16
BAND_BLOCK = 128
N_BUCKETS = 32
BUCKET_MAX_DIST = 128
TOTAL_HEADS = NSA_HEADS + SWA_HEADS + MOBA_HEADS
N_BRANCHES = 3
MIX_WIDTH = NSA_HEADS * HEAD_DIM
D_FF = 4 * D_MODEL
RMS_EPS = 1e-6
NEG_INF = -1e30
FORCE_SCORE = 1e30
TINY = 1e-30

SPLIT_SIZES = (
    NSA_HEADS * HEAD_DIM,
    NSA_KV_HEADS * HEAD_DIM,
    NSA_KV_HEADS * HEAD_DIM,
    NSA_KV_HEADS * HEAD_DIM,
    NSA_KV_HEADS * HEAD_DIM,
    NSA_KV_HEADS * HEAD_DIM,
    NSA_KV_HEADS * HEAD_DIM,
    NSA_HEADS * 3,
    SWA_HEADS * HEAD_DIM,
    SWA_KV_HEADS * HEAD_DIM,
    SWA_KV_HEADS * HEAD_DIM,
    MOBA_HEADS * HEAD_DIM,
    MOBA_KV_HEADS * HEAD_DIM,
    MOBA_KV_HEADS * HEAD_DIM,
    N_BRANCHES * D_MODEL,
)
SPLIT_POINTS = tuple(int(v) for v in np.cumsum(SPLIT_SIZES)[:-1])
D_IN = sum(SPLIT_SIZES)

kernel_name = 'hybrid_nsa_swa_moba_block'


def rms_norm(x, gain):
    x32 = x.astype(jnp.float32)
    y = x32 * lax.rsqrt(jnp.mean(x32 * x32, axis=-1, keepdims=True) + RMS_EPS)
    return (y * gain.astype(jnp.float32)).astype(x.dtype)


def t5_bucket(dist):
    n = jnp.maximum(dist, 0)
    exact = N_BUCKETS // 2
    log_ratio = jnp.log(jnp.maximum(n, 1).astype(jnp.float32) / exact) / math.log(BUCKET_MAX_DIST / exact)
    large = jnp.minimum(exact + (log_ratio * (N_BUCKETS - exact)).astype(jnp.int32), N_BUCKETS - 1)
    return jnp.where(n < exact, n, large)


def masked_softmax(logits, mask, sink=None):
    logits = jnp.where(mask, logits, NEG_INF)
    m = jnp.max(logits, axis=-1, keepdims=True)
    if sink is not None:
        m = jnp.maximum(m, sink)
    p = jnp.where(mask, jnp.exp(logits - m), 0.0)
    denom = jnp.sum(p, axis=-1, keepdims=True)
    if sink is not None:
        denom = denom + jnp.exp(sink - m)
    return p / jnp.maximum(denom, TINY)


def split_heads(t, n_heads):
    return t.reshape(t.shape[0], t.shape[1], n_heads, HEAD_DIM)


def banded_attention(q, k, v, bias_tbl, window, sinks=None):
    b, s, h, d = q.shape
    n_kv = k.shape[2]
    g = h // n_kv
    nb = s // BAND_BLOCK
    pad = -(-(window - 1) // BAND_BLOCK) * BAND_BLOCK
    span = BAND_BLOCK + pad
    key_idx = jnp.arange(nb)[:, None] * BAND_BLOCK + jnp.arange(span)[None, :]
    padding = ((0, 0), (pad, 0), (0, 0), (0, 0))
    kp = jnp.pad(k, padding)[:, key_idx]
    vp = jnp.pad(v, padding)[:, key_idx]
    qb = q.reshape(b, nb, BAND_BLOCK, n_kv, g, d)
    logits = jnp.einsum('bnqkgd,bnlkd->bnkgql', qb, kp, preferred_element_type=jnp.float32) * (d ** -0.5)
    dist = (jnp.arange(BAND_BLOCK)[:, None] + pad) - jnp.arange(span)[None, :]
    bias = jnp.transpose(bias_tbl[t5_bucket(dist)], (2, 0, 1)).reshape(n_kv, g, BAND_BLOCK, span)
    logits = logits + bias.astype(jnp.float32)
    mask = ((dist >= 0) & (dist < window))[None] & (key_idx >= pad)[:, None, :]
    mask = mask[None, :, None, None]
    sink = None
    if sinks is not None:
        sink = sinks.astype(jnp.float32).reshape(n_kv, g, 1, 1)
    p = masked_softmax(logits, mask, sink)
    out = jnp.einsum('bnkgql,bnlkd->bnqkgd', p.astype(v.dtype), vp)
    return out.reshape(b, s, h, d)


def nsa_compress(kv, pos_emb, w1, w2):
    b, s, n_kv, d = kv.shape
    n_cmp = (s - CMP_LEN) // CMP_STRIDE + 1
    idx = jnp.arange(n_cmp)[:, None] * CMP_STRIDE + jnp.arange(CMP_LEN)[None, :]
    blocks = kv[:, idx] + pos_emb[None, None, :, None, :]
    blocks = jnp.moveaxis(blocks, 3, 2).reshape(b, n_cmp, n_kv, CMP_LEN * d)
    return jax.nn.gelu(blocks @ w1) @ w2


def selection_weights(n_cmp, n_sel):
    starts = np.arange(n_cmp)[:, None] * CMP_STRIDE
    blk = np.arange(n_sel)[None, :] * SEL_BLOCK
    shared = np.clip(np.minimum(starts + CMP_LEN, blk + SEL_BLOCK) - np.maximum(starts, blk), 0, None)
    return (shared / CMP_STRIDE).astype(np.float32)


def nsa_selected_attention(qg, k_sel, v_sel, sel_idx, bias_tbl):
    b, s, n_kv, g, d = qg.shape
    n_top = sel_idx.shape[-1]
    n_sel = s // SEL_BLOCK
    n_keys = n_top * SEL_BLOCK
    scale = d ** -0.5
    kb = jnp.moveaxis(k_sel.reshape(b, n_sel, SEL_BLOCK, n_kv, d), 3, 1)
    vb = jnp.moveaxis(v_sel.reshape(b, n_sel, SEL_BLOCK, n_kv, d), 3, 1)
    tbl = jnp.moveaxis(bias_tbl.reshape(N_BUCKETS, n_kv, g), 0, 1)
    n_chunk = s // NSA_Q_CHUNK
    q_chunks = jnp.moveaxis(qg.reshape(b, n_chunk, NSA_Q_CHUNK, n_kv, g, d), 1, 0)
    idx_chunks = jnp.moveaxis(sel_idx.reshape(b, n_kv, n_chunk, NSA_Q_CHUNK, n_top), 2, 0)
    starts = jnp.arange(n_chunk) * NSA_Q_CHUNK
    bi = jnp.arange(b)[:, None, None, None]
    kvi = jnp.arange(n_kv)[None, :, None, None]
    pos_in_blk = jnp.arange(SEL_BLOCK)

    def chunk(xs):
        q_c, idx_c, start = xs
        t_c = start + jnp.arange(NSA_Q_CHUNK)
        k_c = kb[bi, kvi, idx_c]
        v_c = vb[bi, kvi, idx_c]
        dist = t_c[:, None, None] - (idx_c[..., None] * SEL_BLOCK + pos_in_blk)
        bias = jnp.moveaxis(tbl[kvi[..., None], t5_bucket(dist)], -1, 2)
        logits = jnp.einsum('bqkgd,bkqnld->bkgqnl', q_c, k_c, preferred_element_type=jnp.float32) * scale + bias
        mask = (dist >= 0)[:, :, None]
        p = masked_softmax(logits.reshape(b, n_kv, g, NSA_Q_CHUNK, n_keys),
                           mask.reshape(b, n_kv, 1, NSA_Q_CHUNK, n_keys))
        return jnp.einsum('bkgqm,bkqmd->bqkgd', p.astype(v_sel.dtype),
                          v_c.reshape(b, n_kv, NSA_Q_CHUNK, n_keys, d))

    out = lax.map(chunk, (q_chunks, idx_chunks, starts))
    return jnp.moveaxis(out, 0, 1).reshape(b, s, n_kv * g, d)


def nsa_attention(q, k_cmp, v_cmp, k_sel, v_sel, k_win, v_win, gate_logits, cmp_pos, cmp_w1, cmp_w2, bias_tbl):
    b, s, h, d = q.shape
    n_kv = k_cmp.shape[2]
    g = h // n_kv
    scale = d ** -0.5
    qg = q.reshape(b, s, n_kv, g, d)
    t = jnp.arange(s)
    kc = nsa_compress(k_cmp, cmp_pos[0], cmp_w1[0], cmp_w2[0])
    vc = nsa_compress(v_cmp, cmp_pos[1], cmp_w1[1], cmp_w2[1])
    n_cmp = kc.shape[1]
    dist_c = t[:, None] - (jnp.arange(n_cmp) * CMP_STRIDE + CMP_LEN - 1)[None, :]
    bias_c = jnp.moveaxis(bias_tbl[t5_bucket(dist_c)], -1, 0).reshape(n_kv, g, s, n_cmp)
    logits_c = jnp.einsum('bskgd,bckd->bkgsc', qg, kc, preferred_element_type=jnp.float32) * scale + bias_c
    p_cmp = masked_softmax(logits_c, dist_c >= 0)
    o_cmp = jnp.einsum('bkgsc,bckd->bskgd', p_cmp.astype(vc.dtype), vc).reshape(b, s, h, d)
    n_sel = s // SEL_BLOCK
    n_top = min(SEL_TOP_N, n_sel)
    share = jnp.asarray(selection_weights(n_cmp, n_sel))
    importance = jnp.einsum('bkgsc,cj->bksj', p_cmp, share)
    blk = jnp.arange(n_sel)[None, :]
    cur = (t // SEL_BLOCK)[:, None]
    forced = (blk == 0) | (blk == cur) | (blk == cur - 1)
    score = jnp.where(forced, FORCE_SCORE, jnp.where(blk <= cur, importance, NEG_INF))
    _, sel_idx = lax.top_k(score, n_top)
    o_sel = nsa_selected_attention(qg, k_sel, v_sel, sel_idx, bias_tbl)
    o_win = banded_attention(q, k_win, v_win, bias_tbl, NSA_WINDOW)
    gate = jax.nn.sigmoid(gate_logits.astype(jnp.float32)).reshape(b, s, h, 3).astype(q.dtype)
    out = gate[..., 0:1] * o_cmp + gate[..., 1:2] * o_sel + gate[..., 2:3] * o_win
    return out.reshape(b, s, h * d)


def moba_attention(q, k, v, bias_tbl):
    b, s, h, d = q.shape
    n_kv = k.shape[2]
    g = h // n_kv
    scale = d ** -0.5
    n_blk = -(-s // MOBA_BLOCK)
    padding = ((0, 0), (0, n_blk * MOBA_BLOCK - s), (0, 0), (0, 0))
    kb = jnp.moveaxis(jnp.pad(k, padding).reshape(b, n_blk, MOBA_BLOCK, n_kv, d), 3, 1)
    vb = jnp.moveaxis(jnp.pad(v, padding).reshape(b, n_blk, MOBA_BLOCK, n_kv, d), 3, 1)
    qg = q.reshape(b, s, n_kv, g, d)
    tbl = jnp.transpose(bias_tbl.reshape(N_BUCKETS, n_kv, g), (1, 2, 0))
    n_top = min(MOBA_TOP_K, n_blk - 1)
    n_chunk = s // MOBA_Q_CHUNK
    q_chunks = jnp.moveaxis(qg.reshape(b, n_chunk, MOBA_Q_CHUNK, n_kv, g, d), 1, 0)
    starts = jnp.arange(n_chunk) * MOBA_Q_CHUNK
    pos_in_blk = jnp.arange(MOBA_BLOCK)
    bi = jnp.arange(b)[:, None, None, None, None]
    kvi = jnp.arange(n_kv)[None, :, None, None, None]
    gi = jnp.arange(g)[None, None, :, None, None]

    def own_block(q_c, start):
        t_c = start + jnp.arange(MOBA_Q_CHUNK)
        blk = start // MOBA_BLOCK
        k_own = lax.dynamic_index_in_dim(kb, blk, axis=2, keepdims=False)
        v_own = lax.dynamic_index_in_dim(vb, blk, axis=2, keepdims=False)
        dist = t_c[:, None] - (blk * MOBA_BLOCK + pos_in_blk)[None, :]
        logits = jnp.einsum('bqkgd,bkld->bkgql', q_c, k_own, preferred_element_type=jnp.float32) * scale
        logits = logits + tbl[:, :, t5_bucket(dist)]
        mask = jnp.broadcast_to(dist >= 0, logits.shape)
        return logits, mask, v_own, t_c

    if n_top > 0:
        t = jnp.arange(s)
        q_blk = t // MOBA_BLOCK
        k_mean = jnp.mean(kb, axis=3)
        gate = jnp.einsum('bskgd,bknd->bkgsn', qg, k_mean, preferred_element_type=jnp.float32)
        gate = jnp.where(jnp.arange(n_blk)[None, :] < q_blk[:, None], gate, NEG_INF)
        _, sel_idx = lax.top_k(gate, n_top)
        valid = sel_idx < q_blk[:, None]
        idx_chunks = jnp.moveaxis(sel_idx.reshape(b, n_kv, g, n_chunk, MOBA_Q_CHUNK, n_top), 3, 0)
        valid_chunks = jnp.moveaxis(valid.reshape(b, n_kv, g, n_chunk, MOBA_Q_CHUNK, n_top), 3, 0)
        n_sel_keys = n_top * MOBA_BLOCK

        def chunk(xs):
            q_c, start, idx_c, valid_c = xs
            logits_own, mask_own, v_own, t_c = own_block(q_c, start)
            k_sel = kb[bi, kvi, idx_c]
            v_sel = vb[bi, kvi, idx_c]
            dist = t_c[:, None, None] - (idx_c[..., None] * MOBA_BLOCK + pos_in_blk)
            bias = tbl[kvi[..., None], gi[..., None], t5_bucket(dist)]
            logits_sel = jnp.einsum('bqkgd,bkgqnld->bkgqnl', q_c, k_sel, preferred_element_type=jnp.float32) * scale + bias
            logits_sel = logits_sel.reshape(b, n_kv, g, MOBA_Q_CHUNK, n_sel_keys)
            mask_sel = jnp.broadcast_to(valid_c[..., None], dist.shape).reshape(logits_sel.shape)
            p = masked_softmax(jnp.concatenate([logits_sel, logits_own], axis=-1),
                               jnp.concatenate([mask_sel, mask_own], axis=-1)).astype(v.dtype)
            o_sel = jnp.einsum('bkgqm,bkgqmd->bqkgd', p[..., :n_sel_keys],
                               v_sel.reshape(b, n_kv, g, MOBA_Q_CHUNK, n_sel_keys, d))
            o_own = jnp.einsum('bkgql,bkld->bqkgd', p[..., n_sel_keys:], v_own)
            return o_sel + o_own

        out = lax.map(chunk, (q_chunks, starts, idx_chunks, valid_chunks))
    else:
        def chunk(xs):
            q_c, start = xs
            logits_own, mask_own, v_own, _ = own_block(q_c, start)
            p = masked_softmax(logits_own, mask_own).astype(v.dtype)
            return jnp.einsum('bkgql,bkld->bqkgd', p, v_own)

        out = lax.map(chunk, (q_chunks, starts))
    return jnp.moveaxis(out, 0, 1).reshape(b, s, h, d)


def setup_inputs(seed: int = 0) -> dict:
    key = jax.random.key(seed)
    ks = jax.random.split(key, 14)

    def nrm(k, shape, scale):
        return jax.random.normal(k, shape, jnp.float32) * scale

    out_scale = (2 * DEPTH) ** -0.5
    return {
        'x': nrm(ks[0], (BATCH, SEQ, D_MODEL), 1.0),
        'w_in': nrm(ks[1], (DEPTH, D_MODEL, D_IN), D_MODEL ** -0.5),
        'cmp_pos': nrm(ks[2], (DEPTH, 2, CMP_LEN, HEAD_DIM), 0.1),
        'cmp_w1': nrm(ks[3], (DEPTH, 2, CMP_LEN * HEAD_DIM, CMP_HIDDEN), (CMP_LEN * HEAD_DIM) ** -0.5),
        'cmp_w2': nrm(ks[4], (DEPTH, 2, CMP_HIDDEN, HEAD_DIM), CMP_HIDDEN ** -0.5),
        'swa_sinks': nrm(ks[5], (DEPTH, SWA_HEADS), 1.0),
        'w_branch': nrm(ks[6], (DEPTH, N_BRANCHES, MIX_WIDTH, D_MODEL), MIX_WIDTH ** -0.5),
        'w_out': nrm(ks[7], (DEPTH, D_MODEL, D_MODEL), D_MODEL ** -0.5 * out_scale),
        'w_mlp_in': nrm(ks[8], (DEPTH, D_MODEL, D_FF), D_MODEL ** -0.5),
        'w_mlp_out': nrm(ks[9], (DEPTH, D_FF, D_MODEL), D_FF ** -0.5 * out_scale),
        'norm_mix': 1.0 + nrm(ks[10], (DEPTH, D_MODEL), 0.02),
        'norm_mlp': 1.0 + nrm(ks[11], (DEPTH, D_MODEL), 0.02),
        'norm_final': 1.0 + nrm(ks[12], (D_MODEL,), 0.02),
        'rel_bias': nrm(ks[13], (N_BUCKETS, TOTAL_HEADS), 0.5),
    }


def reference(x, w_in, cmp_pos, cmp_w1, cmp_w2, swa_sinks, w_branch, w_out, w_mlp_in, w_mlp_out,
              norm_mix, norm_mlp, norm_final, rel_bias):
    b, s, _ = x.shape
    bias_a = rel_bias[:, :NSA_HEADS]
    bias_b = rel_bias[:, NSA_HEADS:NSA_HEADS + SWA_HEADS]
    bias_c = rel_bias[:, NSA_HEADS + SWA_HEADS:]
    for layer in range(DEPTH):
        h = rms_norm(x, norm_mix[layer])
        (a_q, a_kc, a_vc, a_ks, a_vs, a_kw, a_vw, a_gate,
         b_q, b_k, b_v, c_q, c_k, c_v, merge_gate) = jnp.split(h @ w_in[layer], SPLIT_POINTS, axis=-1)
        o_a = nsa_attention(split_heads(a_q, NSA_HEADS),
                            split_heads(a_kc, NSA_KV_HEADS), split_heads(a_vc, NSA_KV_HEADS),
                            split_heads(a_ks, NSA_KV_HEADS), split_heads(a_vs, NSA_KV_HEADS),
                            split_heads(a_kw, NSA_KV_HEADS), split_heads(a_vw, NSA_KV_HEADS),
                            a_gate, cmp_pos[layer], cmp_w1[layer], cmp_w2[layer], bias_a)
        o_b = banded_attention(split_heads(b_q, SWA_HEADS), split_heads(b_k, SWA_KV_HEADS),
                               split_heads(b_v, SWA_KV_HEADS), bias_b, SWA_WINDOW,
                               swa_sinks[layer]).reshape(b, s, MIX_WIDTH)
        o_c = moba_attention(split_heads(c_q, MOBA_HEADS), split_heads(c_k, MOBA_KV_HEADS),
                             split_heads(c_v, MOBA_KV_HEADS), bias_c).reshape(b, s, MIX_WIDTH)
        branches = jnp.stack([o_a, o_b, o_c], axis=2)
        y = jnp.einsum('bsmc,mcd->bsmd', branches, w_branch[layer])
        gate = jax.nn.sigmoid(merge_gate.reshape(b, s, N_BRANCHES, D_MODEL))
        x = x + jnp.sum(gate * y, axis=2) @ w_out[layer]
        h = rms_norm(x, norm_mlp[layer])
        x = x + jnp.square(jax.nn.relu(h @ w_mlp_in[layer])) @ w_mlp_out[layer]
    return rms_norm(x, norm_final)
```

```python
import math
from contextlib import ExitStack

import numpy as np
import concourse.bass as bass
import concourse.mybir as mybir
from concourse.bass_utils import run_bass_kernel_spmd

F32 = mybir.dt.float32
BF16 = mybir.dt.bfloat16
AF = mybir.ActivationFunctionType
ALU = mybir.AluOpType
AX = mybir.AxisListType

DEPTH = 4
D = 2048
DIN = 11800
DFF = 8192
NT = 1024
SCALE = 128 ** -0.5
NEG = -1e30
NBUF = 2


class Sched:
    ENGS = ("pe", "dve", "act", "pool", "sp")
    HANDLES = {"pe": "tensor", "dve": "vector", "act": "scalar", "pool": "gpsimd", "sp": "sync"}

    def __init__(self, nc, stack):
        self.nc = nc
        self.stack = stack
        self.prog = {e: [] for e in self.ENGS}
        self.cnt = {}
        self.semh = {}
        self.seen = {e: {} for e in self.ENGS}
        self.last_w = {}
        self.readers = {}
        self.n_ops = 0

    def _sem(self, name):
        if name not in self.semh:
            self.semh[name] = self.stack.enter_context(self.nc.semaphore(name))
            self.cnt[name] = 0
        return name

    def _deps(self, eng, group, reads, writes):
        need = {}

        def add(tok):
            if tok[1] > need.get(tok[0], 0):
                need[tok[0]] = tok[1]

        for k in reads:
            w = self.last_w.get(k)
            if w is not None:
                add(w)
        for k in writes:
            w = self.last_w.get(k)
            if w is not None and w[2] != group:
                add(w)
            for r in self.readers.get(k, ()):
                if r[2] != group:
                    add(r)
        waits = []
        seen = self.seen[eng]
        for sem, val in need.items():
            if seen.get(sem, 0) < val:
                seen[sem] = val
                waits.append((sem, val))
        return waits

    def _commit(self, tok, reads, writes):
        for k in writes:
            self.last_w[k] = tok
            self.readers[k] = []
        for k in reads:
            self.readers.setdefault(k, []).append(tok)

    def op(self, eng, fn, reads=(), writes=()):
        reads = tuple(reads); writes = tuple(writes)
        waits = self._deps(eng, eng, reads, writes)
        sem = self._sem("E_" + eng)
        self.cnt[sem] += 1
        self._commit((sem, self.cnt[sem], eng), reads, writes)
        self.prog[eng].append((waits, fn, sem, 1))
        self.n_ops += 1

    def dma(self, eng, fn, sem, reads=(), writes=(), inc=16):
        reads = tuple(reads); writes = tuple(writes)
        sem = self._sem("D_" + sem)
        waits = self._deps(eng, sem, reads, writes)
        self.cnt[sem] += inc
        self._commit((sem, self.cnt[sem], sem), reads, writes)
        self.prog[eng].append((waits, fn, sem, inc))
        self.n_ops += 1

    def barrier(self):
        for eng in self.ENGS:
            waits = []
            for sem, val in self.cnt.items():
                if val > 0 and self.seen[eng].get(sem, 0) < val:
                    self.seen[eng][sem] = val
                    waits.append((sem, val))
            if waits:
                self.prog[eng].append((waits, None, None, 0))
        self.last_w = {}
        self.readers = {}

    def flush(self):
        nc = self.nc
        semh = self.semh
        if not any(self.prog.values()):
            return
        with nc.Block() as block:
            for eng in self.ENGS:
                prog = self.prog[eng]
                if not prog:
                    continue

                def body(e, prog=prog):
                    for waits, fn, sem, inc in prog:
                        for s, v in waits:
                            e.wait_ge(semh[s], v)
                        if fn is not None:
                            fn(e).then_inc(semh[sem], inc)

                getattr(block, self.HANDLES[eng])(body)
        self.prog = {e: [] for e in self.ENGS}


def _bucket(dist):
    n = np.maximum(dist, 0)
    lr = np.log(np.maximum(n, 1).astype(np.float32) / np.float32(16)) / np.float32(math.log(8.0))
    large = np.minimum(16 + (lr * np.float32(16)).astype(np.int32), 31)
    return np.where(n < 16, n, large).astype(np.int64)


def _bias_tbl(tbl_ext, heads, dist, masked):
    idx = np.where(masked, 32, _bucket(dist))
    out = tbl_ext[idx][:, :, heads]
    return np.ascontiguousarray(np.transpose(out, (0, 2, 1)))


def host_tables(rel_bias, j):
    tbl_ext = np.concatenate([rel_bias.astype(np.float32), np.full((1, 24), NEG, np.float32)], axis=0)
    r = np.arange(128)[:, None]
    t = {}
    kk = np.arange(384)[None, :]
    dist = (j + 1) * 128 + r - kk
    t["nb_sel"] = _bias_tbl(tbl_ext, np.arange(0, 8), dist, dist < 0)
    t["nb_moba"] = _bias_tbl(tbl_ext, np.arange(16, 24), dist, dist < 0)
    t["nb_swa"] = _bias_tbl(tbl_ext, np.arange(8, 16), dist, (dist < 0) | (dist >= 128))
    kk = np.arange(768)[None, :]
    dist = (j + 4) * 128 + r - kk
    t["nb_win"] = _bias_tbl(tbl_ext, np.arange(0, 8), dist, (dist < 0) | (dist >= 512))
    t["cfar"] = np.ascontiguousarray(np.broadcast_to(tbl_ext[31][None, :], (128, 24)))
    bc = np.zeros((8, 128, 8, 128), np.float32)
    c = np.arange(128)[None, :]
    for s in range(8):
        dist = (2 * s + j) * 128 + r - 16 * c - 31
        bc[s] = _bias_tbl(tbl_ext, np.arange(0, 8), dist, (dist < 0) | (c >= 127))
    t["bias_c"] = bc
    keep = np.zeros((128, 8, 32), np.float32)
    add = np.zeros((128, 8, 32), np.float32)
    blk = np.arange(32)[None, :]
    for s in range(8):
        cur = 4 * s + 2 * j + (np.arange(128)[:, None] >= 64)
        a = np.zeros((128, 32), np.float32)
        a = np.where(blk > cur, np.float32(-1e30), a)
        a = np.where(blk == cur - 1, np.float32(1e30), a)
        a = np.where(blk == cur, np.float32(2e30), a)
        a = np.where(blk == 0, np.float32(3e30), a)
        add[:, s, :] = a
        keep[:, s, :] = (a == 0)
    t["selkeep"] = keep
    t["seladd"] = add
    return t


def host_consts():
    c = {}
    starts = np.arange(127)[:, None] * 16
    blk = np.arange(32)[None, :] * 64
    shared = np.clip(np.minimum(starts + 32, blk + 64) - np.maximum(starts, blk), 0, None)
    sh = np.zeros((128, 32), np.float32)
    sh[:127] = (shared / 16).astype(np.float32)
    c["share"] = sh
    c["ident"] = np.eye(128, dtype=np.float32)
    n = np.arange(8)[None, :]
    s = np.arange(8)[:, None]
    mv = (n < s).astype(np.float32)
    c["mvalid"] = np.ascontiguousarray(np.broadcast_to(mv[None], (128, 8, 8)))
    c["madd"] = np.ascontiguousarray(np.broadcast_to(np.where(n < s, 0.0, NEG).astype(np.float32)[None], (128, 8, 8)))
    c["mown"] = np.ascontiguousarray(np.broadcast_to((n == s).astype(np.float32)[None], (128, 8, 8)))
    return c


class Prog:
    def __init__(self, mode, layers, final_norm):
        self.mode = mode
        self.layers = layers
        self.LW = len(layers)
        self.final_norm = final_norm
        self.nc = bass.Bass("TRN2", target_bir_lowering=False)
        self.top = ExitStack()
        self.S = Sched(self.nc, self.top)
        self.psn = 0
        self.uid = 0

    def dram(self, name, shape, dt, kind="Internal"):
        return self.nc.dram_tensor(name, list(shape), dt, kind=kind).ap()

    def sb(self, st, name, shape, dt):
        self.uid += 1
        return st.enter_context(self.nc.sbuf_tensor(f"{name}_{self.uid}", list(shape), dt))

    def ktloc(self, t, k):
        if isinstance(self.KT_loc, list):
            return self.KT_loc[t // 3][t % 3, k]
        return self.KT_loc[t, k]

    def ktall(self, j, t, k):
        if isinstance(self.KT_all, list):
            return self.KT_all[t // 3][j, t % 3, k]
        return self.KT_all[j, t, k]

    def ps(self):
        i = self.psn % 8
        self.psn += 1
        return self.psb[i], ("ps", i)

    def declare(self):
        nc = self.nc
        ext_in = "ExternalInput"
        self.xT_in = self.dram("xT", [D, NT], F32, ext_in)
        mode = self.mode
        hasA = mode in ("A", "fused")
        hasB = mode in ("B", "BCD", "fused")
        hasCD = mode in ("BCD", "fused")
        if hasA:
            self.w_in = self.dram("w_in", [self.LW, D, DIN], F32, ext_in)
        if hasB:
            self.posT = self.dram("posT", [self.LW, 2, 128, 32], F32, ext_in)
            self.cmp_w1 = self.dram("cmp_w1", [self.LW, 2, 4096, 256], F32, ext_in)
            self.cmp_w2 = self.dram("cmp_w2", [self.LW, 2, 256, 128], F32, ext_in)
        if hasCD:
            self.w_branch = self.dram("w_branch", [self.LW, 3, 1024, D], F32, ext_in)
            self.w_out = self.dram("w_out", [self.LW, D, D], F32, ext_in)
            self.w_mlp_in = self.dram("w_mlp_in", [self.LW, D, DFF], F32, ext_in)
            self.w_mlp_out = self.dram("w_mlp_out", [self.LW, DFF, D], F32, ext_in)
        self.gam_in = self.dram("gam", [128, (2 * self.LW + 1) * 16], F32, ext_in)
        self.sinks_in = self.dram("sinks", [128, self.LW * 8], F32, ext_in)
        self.tb = {}
        for name, shape in (("nb_sel", [128, 8, 384]), ("nb_moba", [128, 8, 384]), ("nb_swa", [128, 8, 384]),
                            ("nb_win", [128, 8, 768]), ("cfar", [128, 24]), ("bias_c", [8, 128, 8, 128]),
                            ("selkeep", [128, 8, 32]), ("seladd", [128, 8, 32]), ("share", [128, 32]),
                            ("ident", [128, 128]), ("mvalid", [128, 8, 8]), ("madd", [128, 8, 8]),
                            ("mown", [128, 8, 8])):
            self.tb[name] = self.dram(name, shape, F32, ext_in)
        fused = self.mode == "fused"
        kA = "Internal" if fused else ("ExternalOutput" if self.mode == "A" else "ExternalInput")
        self.QT_d = self.dram("QT_d", [24, 128, NT], BF16, kA)
        self.AG_d = self.dram("AG_d", [NT, 24], F32, kA)
        if mode != "B":
            self.G_d = self.dram("G_d", [48, 128, NT], F32, kA)
        if self.mode in ("A", "fused"):
            if fused:
                self.KT_loc2 = [self.dram(f"KT_loc{g}", [3 * 2 * 128, NT], BF16, "Internal") for g in range(2)]
                self.V_loc2 = self.dram("V_loc", [4 * NT, 256], BF16, "Internal")
                self.KT_loc = [a.rearrange("(t k p) l -> t k p l", t=3, k=2) for a in self.KT_loc2]
                self.V_loc = self.V_loc2.rearrange("(t l) n -> t l n", t=4)
            else:
                self.KT_loc = self.dram("KT_loc", [6, 2, 128, NT], BF16, "ExternalOutput")
                self.V_loc = self.dram("V_loc", [4, NT, 256], BF16, "ExternalOutput")
        if hasB:
            if fused:
                self.KT_all2 = [self.dram(f"KT_all{g}", [2 * 3 * 2 * 128, NT], BF16, "Internal") for g in range(2)]
                self.V_all2 = self.dram("V_all", [2 * 4 * NT, 256], BF16, "Internal")
                self.KT_all = [a.rearrange("(j t k p) l -> j t k p l", j=2, t=3, k=2) for a in self.KT_all2]
                self.V_all = self.V_all2.rearrange("(j t l) n -> j t l n", j=2, t=4)
            else:
                self.KT_all = self.dram("KT_all", [2, 6, 2, 128, NT], BF16, "ExternalInput")
                self.V_all = self.dram("V_all", [2, 4, NT, 256], BF16, "ExternalInput")
            self.OT_d = self.dram("OT_d", [24, 128, NT], BF16, "ExternalOutput" if mode == "B" else "Internal")
            if mode == "B":
                self.dbg = self.dram("dbg", [8, 128, 8, 16], F32, "ExternalOutput")
                self.dbg2 = self.dram("dbg2", [2, 128, 8], F32, "ExternalOutput")
        if hasCD:
            self.xT_out = self.dram("xT_out", [D, NT], F32, "ExternalOutput")

    def persistent(self):
        st = self.top
        nc = self.nc
        S = self.S
        self.XT = self.sb(st, "XT", [128, 16, NT], F32)
        self.gam = self.sb(st, "gam", [128, (2 * self.LW + 1) * 16], F32)
        self.identf = self.sb(st, "identf", [128, 128], F32)
        self.identb = self.sb(st, "identb", [128, 128], BF16)
        self.onesf = self.sb(st, "onesf", [128, 128], F32)
        self.psb = [st.enter_context(nc.psum_tensor(f"psb{i}", [128, 512], F32)) for i in range(8)]
        xin = self.xT_in.rearrange("(c p) l -> p c l", p=128)
        for c4 in range(4):
            S.dma("sp", lambda q, c4=c4: q.dma_start(out=self.XT[:, c4 * 4:(c4 + 1) * 4, :], in_=xin[:, c4 * 4:(c4 + 1) * 4, :]),
                  f"ld_x{c4}", writes=[("XT", c) for c in range(c4 * 4, c4 * 4 + 4)])
        S.dma("sp", lambda q: q.dma_start(out=self.gam[:], in_=self.gam_in[:, :]), "ld_gam", writes=["gam"])
        S.dma("sp", lambda q: q.dma_start(out=self.identf[:], in_=self.tb["ident"][:, :]), "ld_idf", writes=["identf"])
        S.dma("pool", lambda q: q.dma_start(out=self.identb[:], in_=self.tb["ident"][:, :]), "ld_c2", writes=["identb"])
        S.op("dve", lambda q: q.memset(self.onesf[:], 1.0), writes=["onesf"])

    def rmsnorm(self, st, nidx, HT, htkey, out_f32=None):
        S = self.S
        RS = self.sb(st, "RS", [128, NT], F32)
        SQ = [self.sb(st, f"SQ{i}", [128, 512], F32) for i in range(2)]
        for half in range(2):
            cols = slice(half * 512, (half + 1) * 512)
            bank, bkey = self.ps()
            for c in range(16):
                sq = SQ[c % 2]
                sqk = ("SQ", c % 2)
                S.op("act", lambda q, sq=sq, c=c, cols=cols: q.activation(out=sq[:], in_=self.XT[:, c, cols], func=AF.Square),
                     reads=[("XT", c)], writes=[sqk])
                S.op("pe", lambda q, sq=sq, c=c, bank=bank: q.matmul(bank[:], self.onesf[:], sq[:], start=(c == 0), stop=(c == 15)),
                     reads=[sqk, "onesf"], writes=[bkey])
            S.op("dve", lambda q, bank=bank, cols=cols: q.tensor_scalar(out=RS[:, cols], in0=bank[:], scalar1=1.0 / D, scalar2=1e-6,
                                                                       op0=ALU.mult, op1=ALU.add),
                 reads=[bkey], writes=[("RS", half)])
            S.op("act", lambda q, cols=cols: q.activation(out=RS[:, cols], in_=RS[:, cols], func=AF.Sqrt),
                 reads=[("RS", half)], writes=[("RS", half)])
            S.op("dve", lambda q, cols=cols: q.reciprocal(out=RS[:, cols], in_=RS[:, cols]),
                 reads=[("RS", half)], writes=[("RS", half)])
        for c in range(16):
            g = self.gam[:, nidx * 16 + c:nidx * 16 + c + 1]
            dst = HT[:, c, :] if out_f32 is None else out_f32[:, c, :]
            S.op("dve", lambda q, c=c, g=g, dst=dst: q.scalar_tensor_tensor(out=dst, in0=self.XT[:, c, :], scalar=g, in1=RS[:],
                                                                            op0=ALU.mult, op1=ALU.mult),
                 reads=[("XT", c), ("RS", 0), ("RS", 1), "gam"], writes=[(htkey, c)])

    def phase_A(self, layer):
        S = self.S
        nc = self.nc
        with ExitStack() as st:
            HT = self.sb(st, "HT", [128, 16, NT], BF16)
            self.rmsnorm(st, layer * 2, HT, "HT")
            htr = [("HT", c) for c in range(16)]
            NSL = 4
            slots = [self.sb(st, f"wsl{i}", [128, 16, 512], BF16) for i in range(NSL)]
            FMo = [self.sb(st, f"fmo{i}", [128, NT], BF16) for i in range(3)]
            GTo = [self.sb(st, f"gto{i}", [128, NT], F32) for i in range(3)]
            TMo = [self.sb(st, f"tmo{i}", [128, 8, 256], BF16) for i in range(2)]
            AGo = self.sb(st, "ago", [128, 8, 24], F32)
            blocks = []

            def fm(sub, dst, sig=False):
                return ("fm", sub, dst, sig)

            blocks.append((0, 512, [fm(u * 128, self.QT_d[u]) for u in range(4)]))
            blocks.append((512, 512, [fm(u * 128, self.QT_d[4 + u]) for u in range(4)]))
            blocks.append((1024, 512, [fm(0, self.ktloc(0, 0)), fm(128, self.ktloc(0, 1)),
                                       fm(256, self.ktloc(1, 0)), fm(384, self.ktloc(1, 1))]))
            blocks.append((1536, 512, [fm(0, self.ktloc(2, 0)), fm(128, self.ktloc(2, 1)), ("tm", 256, 256, self.V_loc[0], False)]))
            blocks.append((2048, 512, [fm(0, self.ktloc(3, 0)), fm(128, self.ktloc(3, 1)), ("tm", 256, 256, self.V_loc[1], False)]))
            blocks.append((2560, 24, [("tm", 0, 24, self.AG_d, True)]))
            blocks.append((2584, 512, [fm(u * 128, self.QT_d[8 + u]) for u in range(4)]))
            blocks.append((3096, 512, [fm(u * 128, self.QT_d[12 + u]) for u in range(4)]))
            blocks.append((3608, 512, [fm(0, self.ktloc(4, 0)), fm(128, self.ktloc(4, 1)), ("tm", 256, 256, self.V_loc[2], False)]))
            blocks.append((4120, 512, [fm(u * 128, self.QT_d[16 + u]) for u in range(4)]))
            blocks.append((4632, 512, [fm(u * 128, self.QT_d[20 + u]) for u in range(4)]))
            blocks.append((5144, 512, [fm(0, self.ktloc(5, 0)), fm(128, self.ktloc(5, 1)), ("tm", 256, 256, self.V_loc[3], False)]))
            for i in range(12):
                blocks.append((5656 + i * 512, 512, [fm(u * 128, self.G_d[i * 4 + u], True) for u in range(4)]))

            kv_idx = [2, 3, 4, 8, 11]
            blocks = [blocks[i] for i in kv_idx] + [b for i, b in enumerate(blocks) if i not in kv_idx]
            n_kv = len(kv_idx)
            kvkeys = []

            def load(bi):
                col0, ncols, _ = blocks[bi]
                slot = slots[bi % NSL]
                src = self.w_in[layer, :, col0:col0 + ncols].rearrange("(k p) n -> p k n", p=128)
                for kh in range(2):
                    S.dma("pool", lambda q, slot=slot, src=src, kh=kh, ncols=ncols:
                          q.dma_start(out=slot[:, kh * 8:(kh + 1) * 8, 0:ncols], in_=src[:, kh * 8:(kh + 1) * 8, :]),
                          f"wslA{bi % NSL}", writes=[("wsl", bi % NSL)])

            PF = 2
            for bi in range(min(PF, len(blocks))):
                load(bi)
            ofm = 0
            ogt = 0
            otm = 0
            for bi, (col0, ncols, units) in enumerate(blocks):
                if bi + PF < len(blocks):
                    load(bi + PF)
                slot = slots[bi % NSL]
                skey = ("wsl", bi % NSL)
                for u in units:
                    if u[0] == "fm":
                        _, sub, dst, sig = u
                        if sig:
                            ob = GTo[ogt % 3]; okey = ("gto", ogt % 3); ogt += 1
                        else:
                            ob = FMo[ofm % 3]; okey = ("fmo", ofm % 3); ofm += 1
                        for half in range(2):
                            cols = slice(half * 512, (half + 1) * 512)
                            bank, bkey = self.ps()
                            for k in range(16):
                                S.op("pe", lambda q, bank=bank, slot=slot, k=k, sub=sub, cols=cols:
                                     q.matmul(bank[:], slot[:, k, sub:sub + 128], HT[:, k, cols], start=(k == 0), stop=(k == 15)),
                                     reads=[skey, ("HT", k)], writes=[bkey])
                            S.op("act", lambda q, bank=bank, ob=ob, cols=cols, sig=sig:
                                 q.activation(out=ob[:, cols], in_=bank[:], func=(AF.Sigmoid if sig else AF.Copy)),
                                 reads=[bkey], writes=[okey + (half,)])
                        S.dma("sp", lambda q, ob=ob, dst=dst: q.dma_start(out=dst, in_=ob[:]), "stA_" + "_".join(str(x) for x in okey),
                              reads=[okey + (0,), okey + (1,)], writes=([("KV_out", len(kvkeys))] if bi < n_kv else []))
                        if bi < n_kv:
                            kvkeys.append(("KV_out", len(kvkeys)))
                    else:
                        _, sub, n, dst, sig = u
                        if sig:
                            ob = AGo; okey = ("ago",)
                        else:
                            ob = TMo[otm % 2]; okey = ("tmo", otm % 2); otm += 1
                        for s in range(8):
                            bank, bkey = self.ps()
                            for k in range(16):
                                S.op("pe", lambda q, bank=bank, slot=slot, k=k, sub=sub, n=n, s=s:
                                     q.matmul(bank[:, 0:n], HT[:, k, s * 128:(s + 1) * 128], slot[:, k, sub:sub + n],
                                              start=(k == 0), stop=(k == 15)),
                                     reads=[skey, ("HT", k)], writes=[bkey])
                            S.op("act", lambda q, bank=bank, ob=ob, s=s, n=n, sig=sig:
                                 q.activation(out=ob[:, s, 0:n], in_=bank[:, 0:n], func=(AF.Sigmoid if sig else AF.Copy)),
                                 reads=[bkey], writes=[okey + (s,)])
                        dview = dst.rearrange("(s p) n -> p s n", p=128)
                        S.dma("sp", lambda q, ob=ob, dview=dview, n=n: q.dma_start(out=dview, in_=ob[:, :, 0:n]), "stA_" + "_".join(str(x) for x in okey),
                              reads=[okey + (s,) for s in range(8)], writes=([("KV_out", len(kvkeys))] if bi < n_kv else []))
                        if bi < n_kv:
                            kvkeys.append(("KV_out", len(kvkeys)))
                if bi == n_kv - 1 and self.mode == "fused":
                    self.exchange(kvkeys)
            S.barrier()
            S.flush()

    def compress(self, layer, kcT, vcc):
        S = self.S
        with ExitStack() as st:
            KC = [self.sb(st, f"KC{i}", [128, 2048], BF16) for i in range(2)]
            W1 = [self.sb(st, f"cw1{i}", [128, 32, 256], BF16) for i in range(2)]
            W2 = [self.sb(st, f"cw2{i}", [128, 2, 128], BF16) for i in range(2)]
            posT = self.sb(st, "posT", [128, 2, 32], F32)
            posB = [self.sb(st, f"posB{i}", [128, 32, 127], BF16) for i in range(2)]
            G = [self.sb(st, f"Gc{i}", [128, 2, 127], BF16) for i in range(2)]
            xs = [self.sb(st, f"gx{i}", [128, 127], F32) for i in range(2)]
            t1 = [self.sb(st, f"gt{i}", [128, 127], F32) for i in range(2)]
            for kv in range(2):
                S.dma("sp", lambda q, kv=kv: q.dma_start(out=posT[:, kv, :], in_=self.posT[layer, kv]), f"ld_pos{kv}", writes=[("posT", kv)])
                w1src = self.cmp_w1[layer, kv].rearrange("(l p) n -> p l n", p=128)
                for lh in range(2):
                    S.dma("pool", lambda q, kv=kv, lh=lh, w1src=w1src: q.dma_start(out=W1[kv][:, lh * 16:(lh + 1) * 16, :], in_=w1src[:, lh * 16:(lh + 1) * 16, :]),
                          f"cw1_{kv}", writes=[("cw1", kv)])
                S.dma("pool", lambda q, kv=kv: q.dma_start(out=W2[kv][:], in_=self.cmp_w2[layer, kv].rearrange("(c p) n -> p c n", p=128)),
                      f"cw2_{kv}", writes=[("cw2", kv)])
            it = 0
            for kv in range(2):
                for kvh in range(2):
                    i = it % 2
                    it += 1
                    kc = KC[i]
                    kcv = kc[:].rearrange("p (s j r) -> p s j r", s=8, j=2)
                    for j in range(2):
                        src = self.ktall(j, kv, kvh).rearrange("p (s r) -> p s r", r=128)
                        S.dma("sp", lambda q, kcv=kcv, j=j, src=src: q.dma_start(out=kcv[:, :, j, :], in_=src), f"ld_kc{i}", writes=[("KC", i)])
                    if kvh == 0:
                        S.op("dve", lambda q, kv=kv: q.tensor_copy(out=posB[kv][:], in_=posT[:, kv, :].unsqueeze(2).broadcast_to([128, 32, 127])),
                             reads=[("posT", kv)], writes=[("posB", kv)])
                    for hc in range(2):
                        bank, bkey = self.ps()
                        for l in range(32):
                            S.op("pe", lambda q, bank=bank, kv=kv, l=l, hc=hc, kc=kc: q.matmul(bank[:, 0:127], W1[kv][:, l, hc * 128:(hc + 1) * 128], kc[:, l:l + 2017:16],
                                                                                         start=(l == 0), stop=False),
                                 reads=[("cw1", kv), ("KC", i)], writes=[bkey])
                        for l in range(32):
                            S.op("pe", lambda q, bank=bank, kv=kv, l=l, hc=hc: q.matmul(bank[:, 0:127], W1[kv][:, l, hc * 128:(hc + 1) * 128], posB[kv][:, l, :],
                                                                                    start=False, stop=(l == 31)),
                                 reads=[("cw1", kv), ("posB", kv)], writes=[bkey])
                        x_, t_ = xs[hc], t1[hc]
                        xk, tk = ("gx", hc), ("gt", hc)
                        S.op("act", lambda q, bank=bank, x_=x_: q.activation(out=x_[:], in_=bank[:, 0:127], func=AF.Copy), reads=[bkey], writes=[xk])
                        S.op("dve", lambda q, x_=x_, t_=t_: q.tensor_tensor(out=t_[:], in0=x_[:], in1=x_[:], op=ALU.mult), reads=[xk], writes=[tk])
                        S.op("dve", lambda q, t_=t_: q.tensor_scalar(out=t_[:], in0=t_[:], scalar1=0.044715, scalar2=1.0, op0=ALU.mult, op1=ALU.add), reads=[tk], writes=[tk])
                        S.op("dve", lambda q, x_=x_, t_=t_: q.tensor_tensor(out=t_[:], in0=t_[:], in1=x_[:], op=ALU.mult), reads=[tk, xk], writes=[tk])
                        S.op("act", lambda q, t_=t_: q.activation(out=t_[:], in_=t_[:], func=AF.Sigmoid, scale=1.5957691216057308), reads=[tk], writes=[tk])
                        S.op("dve", lambda q, x_=x_, t_=t_, i=i, hc=hc: q.tensor_tensor(out=G[i][:, hc, :], in0=t_[:], in1=x_[:], op=ALU.mult),
                             reads=[tk, xk], writes=[("Gc", i, hc)])
                    bank, bkey = self.ps()
                    if kv == 0:
                        for hc in range(2):
                            S.op("pe", lambda q, bank=bank, hc=hc, i=i, kv=kv: q.matmul(bank[:, 0:127], W2[kv][:, hc, :], G[i][:, hc, :], start=(hc == 0), stop=(hc == 1)),
                                 reads=[("cw2", kv), ("Gc", i, hc)], writes=[bkey])
                        S.op("act", lambda q, bank=bank, kvh=kvh: q.activation(out=kcT[:, kvh, 0:127], in_=bank[:, 0:127], func=AF.Copy),
                             reads=[bkey], writes=[("kcT", kvh)])
                    else:
                        for hc in range(2):
                            S.op("pe", lambda q, bank=bank, hc=hc, i=i, kv=kv: q.matmul(bank[0:127, 0:128], G[i][:, hc, :], W2[kv][:, hc, :], start=(hc == 0), stop=(hc == 1)),
                                 reads=[("cw2", kv), ("Gc", i, hc)], writes=[bkey])
                        S.op("act", lambda q, bank=bank, kvh=kvh: q.activation(out=vcc[0:127, kvh, 0:128], in_=bank[0:127, 0:128], func=AF.Copy),
                             reads=[bkey], writes=[("vcc", kvh)])
            S.barrier()
            S.flush()

    def attn(self, R, qt, qkey, KT, kkey, V, vkey, kt0, kt1, near_start, nb, nbkey, nb_off, cfar_ap,
             mask=None, mkey=None, mblk=None, sink=None, cmp_slot=None, cmp_first=False, fin=None):
        S = self.S
        i = R["ac"] % NBUF
        k8 = R["ac"] % 8
        R["ac"] += 1
        L, Pb, PT, sm = R["L"][i], R["Pb"][i], R["PT"][i], R["sm"]
        Lk, Pk, Tk = ("L", i), ("Pb", i), ("PT", i)
        smk = ("sm", k8)
        n = (kt1 - kt0) * 128
        nfar = (near_start - kt0) * 128
        mbk = None
        if mask is not None and nfar > 0:
            mb = R["mb"]
            mbk = ("mb", k8)
            nfb = nfar // mblk
            S.op("dve", lambda q: q.tensor_scalar(out=mb[:, k8, 0:nfb], in0=mask[:, 0:nfb], scalar1=cfar_ap, scalar2=None, op0=ALU.add),
                 reads=[mkey, "cfar"], writes=[mbk])
        for c0 in range(0, n, 512):
            w = min(512, n - c0)
            bank, bkey = self.ps()
            S.op("pe", lambda q, bank=bank, w=w, c0=c0: q.matmul(bank[:, 0:w], qt, KT[:, kt0 * 128 + c0:kt0 * 128 + c0 + w], start=True, stop=True),
                 reads=[qkey, kkey], writes=[bkey])
            a, b = c0, min(c0 + w, nfar)
            if b > a:
                if mask is None:
                    S.op("dve", lambda q, bank=bank, a=a, b=b, c0=c0: q.tensor_scalar(out=L[:, a:b], in0=bank[:, a - c0:b - c0], scalar1=SCALE, scalar2=cfar_ap,
                                                                                     op0=ALU.mult, op1=ALU.add),
                         reads=[bkey, "cfar"], writes=[Lk])
                else:
                    nbk = (b - a) // mblk
                    S.op("dve", lambda q, bank=bank, a=a, b=b, c0=c0, nbk=nbk: q.scalar_tensor_tensor(
                        out=L[:, a:b].rearrange("p (b r) -> p b r", r=mblk), in0=bank[:, a - c0:b - c0].rearrange("p (b r) -> p b r", r=mblk), scalar=SCALE,
                        in1=R["mb"][:, k8, a // mblk:a // mblk + nbk].unsqueeze(2).broadcast_to([128, nbk, mblk]), op0=ALU.mult, op1=ALU.add),
                         reads=[bkey, mbk], writes=[Lk])
            a, b = max(c0, nfar), c0 + w
            if b > a:
                S.op("dve", lambda q, bank=bank, a=a, b=b, c0=c0: q.scalar_tensor_tensor(out=L[:, a:b], in0=bank[:, a - c0:b - c0], scalar=SCALE,
                                                                                        in1=nb[:, nb_off + a - nfar:nb_off + b - nfar],
                                                                                        op0=ALU.mult, op1=ALU.add),
                     reads=[bkey, nbkey], writes=[Lk])
        if mask is not None:
            nnb = (n - nfar) // mblk
            lv = L[:, nfar:n].rearrange("p (b r) -> p b r", r=mblk)
            S.op("dve", lambda q: q.tensor_tensor(out=lv, in0=lv, in1=mask[:, nfar // mblk:nfar // mblk + nnb].unsqueeze(2).broadcast_to([128, nnb, mblk]), op=ALU.add),
                 reads=[Lk, mkey], writes=[Lk])
        mk, nk = smk + ("m",), smk + ("negm",)
        S.op("dve", lambda q: q.reduce_max(out=sm[:, k8, 0:1], in_=L[:, 0:n], axis=AX.X), reads=[Lk], writes=[mk])
        clamp = sink if sink is not None else -30000.0
        S.op("dve", lambda q: q.tensor_scalar(out=sm[:, k8, 1:2], in0=sm[:, k8, 0:1], scalar1=clamp, scalar2=-1.0, op0=ALU.max, op1=ALU.mult),
             reads=[mk] + (["sinks"] if sink is not None else []), writes=[nk])
        negm = sm[:, k8, 1:2]

        def s2():
            if cmp_slot is not None:
                P32 = R["P32"]
                psum4 = R["psum4"]
                dk, rk = smk + ("d32",), smk + ("r32",)
                S.op("dve", lambda q: q.memset(sm[:, k8, 6:7], 0.0), writes=[dk])
                S.op("act", lambda q: q.activation(out=P32[:, 0:n], in_=L[:, 0:n], func=AF.Exp, bias=negm, scale=1.0, accum_out=sm[:, k8, 6:7]),
                     reads=[Lk, nk, dk], writes=["P32", dk])
                S.op("dve", lambda q: q.tensor_scalar(out=sm[:, k8, 7:8], in0=sm[:, k8, 6:7], scalar1=1e-30, scalar2=None, op0=ALU.max), reads=[dk], writes=[rk])
                S.op("dve", lambda q: q.reciprocal(out=sm[:, k8, 7:8], in_=sm[:, k8, 7:8]), reads=[rk], writes=[rk])
                p4k = ("psum4", cmp_slot)
                if cmp_first:
                    S.op("dve", lambda q: q.tensor_scalar(out=psum4[:, cmp_slot, :], in0=P32[:, 0:n], scalar1=sm[:, k8, 7:8], scalar2=None, op0=ALU.mult),
                         reads=["P32", rk], writes=[p4k])
                else:
                    S.op("dve", lambda q: q.scalar_tensor_tensor(out=psum4[:, cmp_slot, :], in0=P32[:, 0:n], scalar=sm[:, k8, 7:8], in1=psum4[:, cmp_slot, :],
                                                                 op0=ALU.mult, op1=ALU.add),
                         reads=["P32", rk, p4k], writes=[p4k])
                S.op("pool", lambda q: q.tensor_copy(out=Pb[:, 0:n], in_=P32[:, 0:n]), reads=["P32"], writes=[Pk])
            else:
                S.op("act", lambda q: q.activation(out=Pb[:, 0:n], in_=L[:, 0:n], func=AF.Exp, bias=negm, scale=1.0), reads=[Lk, nk], writes=[Pk])

        ntile = kt1 - kt0

        cl = cmp_slot is not None

        def s3():
            self._attn_s3(Pb, PT, Pk, Tk, ntile)

        def s4():
            self._attn_s4(PT, Tk, V, vkey, kt0, ntile, sink, negm, nk, sm, k8, smk, fin, cl)

        R["pend"].append([s2, s3, s4])
        self.attn_round(R)

    def attn_cmp4(self, R, QT, qkey, s, kvh, kcT, vcc, BC, bckey, AG, OA):
        S = self.S
        i = R["ac"] % NBUF
        k8 = R["ac"] % 8
        R["ac"] += 1
        L, Pb, PT, s4t, P32, psum4 = R["L"][i], R["Pb"][i], R["PT"][i], R["s4"], R["P32"], R["psum4"]
        Lk, Pk, Tk = ("L", i), ("Pb", i), ("PT", i)
        sk = ("s4", k8)
        m4, negm4, den4, r4 = s4t[:, k8, 0:4], s4t[:, k8, 4:8], s4t[:, k8, 8:12], s4t[:, k8, 12:16]
        bank, bkey = self.ps()
        for g in range(4):
            S.op("pe", lambda q, g=g: q.matmul(bank[:, g * 128:(g + 1) * 128], QT[:, g, s * 128:(s + 1) * 128], kcT[:, kvh, :], start=True, stop=True),
                 reads=[qkey, ("kcT", kvh)], writes=[bkey])
        S.op("dve", lambda q: q.scalar_tensor_tensor(out=L[:, 0:512], in0=bank[:], scalar=SCALE, in1=BC[:, kvh * 4:(kvh + 1) * 4, :].rearrange("p g c -> p (g c)"),
                                                     op0=ALU.mult, op1=ALU.add), reads=[bkey, bckey], writes=[Lk])
        S.op("dve", lambda q: q.reduce_max(out=m4, in_=L[:, 0:512].rearrange("p (g c) -> p g c", c=128), axis=AX.X), reads=[Lk], writes=[sk + ("m",)])
        S.op("dve", lambda q: q.tensor_scalar(out=negm4, in0=m4, scalar1=-30000.0, scalar2=-1.0, op0=ALU.max, op1=ALU.mult), reads=[sk + ("m",)], writes=[sk + ("negm",)])
        S.op("dve", lambda q: q.memset(den4, 0.0), writes=[sk + ("den",)])

        def s2():
            for g in range(4):
                S.op("act", lambda q, g=g: q.activation(out=P32[:, g * 128:(g + 1) * 128], in_=L[:, g * 128:(g + 1) * 128], func=AF.Exp, bias=s4t[:, k8, 4 + g:5 + g], scale=1.0,
                                                        accum_out=s4t[:, k8, 8 + g:9 + g]),
                     reads=[Lk, sk + ("negm",), sk + ("den",)], writes=["P32", sk + ("den",)])
            S.op("dve", lambda q: q.tensor_scalar(out=r4, in0=den4, scalar1=1e-30, scalar2=None, op0=ALU.max), reads=[sk + ("den",)], writes=[sk + ("r",)])
            S.op("dve", lambda q: q.reciprocal(out=r4, in_=r4), reads=[sk + ("r",)], writes=[sk + ("r",)])
            lv = L[:, 0:512].rearrange("p (g c) -> p g c", c=128)
            S.op("dve", lambda q: q.tensor_tensor(out=lv, in0=P32[:].rearrange("p (g c) -> p g c", c=128), in1=r4.unsqueeze(2).broadcast_to([128, 4, 128]), op=ALU.mult),
                 reads=["P32", sk + ("r",), Lk], writes=[Lk])
            S.op("dve", lambda q: q.reduce_sum(out=psum4[:, s, :], in_=L[:, 0:512].rearrange("p (g c) -> p c g", c=128), axis=AX.X), reads=[Lk], writes=[("psum4", s)])
            S.op("pool", lambda q: q.tensor_copy(out=Pb[:, 0:512], in_=P32[:]), reads=["P32"], writes=[Pk])

        def s3():
            bk, bkk = self.ps()
            bb = bk[:].bitcast(BF16)
            for g in range(4):
                S.op("pe", lambda q, g=g: q.transpose(bb[:, g * 128:(g + 1) * 128], Pb[:, g * 128:(g + 1) * 128], self.identb[:]), reads=[Pk, "identb"], writes=[bkk])
            S.op("act", lambda q: q.activation(out=PT[:, 0:512], in_=bb[:, 0:512], func=AF.Copy), reads=[bkk], writes=[Tk])

        def s4():
            bo, bok = self.ps()
            for g in range(4):
                S.op("pe", lambda q, g=g: q.matmul(bo[:, g * 128:(g + 1) * 128], PT[:, g * 128:(g + 1) * 128], vcc[:, kvh, 0:128], start=True, stop=True),
                     reads=[Tk, ("vcc", kvh)], writes=[bok])
            S.op("dve", lambda q: q.tensor_tensor(out=m4, in0=r4, in1=AG[:, s, kvh * 12:kvh * 12 + 12:3], op=ALU.mult), reads=[sk + ("r",), "AG"], writes=[sk + ("m",)])
            S.op("dve", lambda q: q.tensor_tensor(out=OA[:, s, :, :], in0=bo[:].rearrange("p (g c) -> p g c", c=128), in1=m4.unsqueeze(2).broadcast_to([128, 4, 128]), op=ALU.mult),
                 reads=[bok, sk + ("m",)], writes=[("OA", s, g) for g in range(4)])

        R["pend"].append([s2, s3, s4])
        self.attn_round(R)

    def attn_round(self, R):
        pend = R["pend"]
        n = len(pend)
        for idx, stage in ((n - 2, 0), (n - 3, 1), (n - 4, 2)):
            if idx >= 0 and pend[idx] is not None:
                pend[idx][stage]()
        if n >= 4:
            pend.pop(0)

    def attn_bubble(self, R):
        R["pend"].append(None)
        self.attn_round(R)

    def attn_marker(self, R, fn):
        noop = lambda: None
        R["pend"].append([noop, noop, fn])
        self.attn_round(R)

    def attn_drain(self, R):
        pend = R["pend"]
        while any(p is not None for p in pend):
            self.attn_bubble(R)
        pend.clear()

    def _attn_s3(self, Pb, PT, Pk, Tk, ntile):
        S = self.S
        for t0 in range(0, ntile, 8):
            tn = min(8, ntile - t0)
            bank, bkey = self.ps()
            bb = bank[:].bitcast(BF16)
            for t in range(t0, t0 + tn):
                S.op("pe", lambda q, bb=bb, t=t, t0=t0: q.transpose(bb[:, (t - t0) * 128:(t - t0 + 1) * 128], Pb[:, t * 128:(t + 1) * 128], self.identb[:]),
                     reads=[Pk, "identb"], writes=[bkey])
            S.op("act", lambda q, bb=bb, t0=t0, tn=tn: q.activation(out=PT[:, t0 * 128:(t0 + tn) * 128], in_=bb[:, 0:tn * 128], func=AF.Copy),
                 reads=[bkey], writes=[Tk])

    def _attn_s4(self, PT, Tk, V, vkey, kt0, ntile, sink, negm, nk, sm, k8, smk, fin, clampden):
        S = self.S
        bank_o, okey = self.ps()
        for t in range(ntile):
            S.op("pe", lambda q, t=t: q.matmul(bank_o[:, 0:129], PT[:, t * 128:(t + 1) * 128], V[:, kt0 + t, 0:129], start=(t == 0), stop=(t == ntile - 1)),
                 reads=[Tk, vkey], writes=[okey])
        dk2, rk2 = smk + ("den",), smk + ("r",)
        if sink is not None:
            ek = smk + ("es",)
            S.op("act", lambda q: q.activation(out=sm[:, k8, 5:6], in_=sink, func=AF.Exp, bias=negm, scale=1.0), reads=[nk, "sinks"], writes=[ek])
            S.op("dve", lambda q: q.tensor_tensor(out=sm[:, k8, 2:3], in0=bank_o[:, 128:129], in1=sm[:, k8, 5:6], op=ALU.add), reads=[okey, ek], writes=[dk2])
            S.op("dve", lambda q: q.reciprocal(out=sm[:, k8, 3:4], in_=sm[:, k8, 2:3]), reads=[dk2], writes=[rk2])
        elif clampden:
            S.op("dve", lambda q: q.tensor_scalar(out=sm[:, k8, 2:3], in0=bank_o[:, 128:129], scalar1=1e-30, scalar2=None, op0=ALU.max), reads=[okey], writes=[dk2])
            S.op("dve", lambda q: q.reciprocal(out=sm[:, k8, 3:4], in_=sm[:, k8, 2:3]), reads=[dk2], writes=[rk2])
        else:
            S.op("dve", lambda q: q.reciprocal(out=sm[:, k8, 3:4], in_=bank_o[:, 128:129]), reads=[okey], writes=[rk2])
        fin(bank_o, okey, sm, k8, smk)

    def phase_B(self, layer):
        S = self.S
        with ExitStack() as st:
            cfar = self.sb(st, "cfar", [128, 24], F32)
            share = self.sb(st, "share", [128, 32], F32)
            selkeep = self.sb(st, "selkeep", [128, 8, 32], F32)
            seladd = self.sb(st, "seladd", [128, 8, 32], F32)
            mvalid = self.sb(st, "mvalid", [128, 8, 8], F32)
            madd = self.sb(st, "madd", [128, 8, 8], F32)
            mown = self.sb(st, "mown", [128, 8, 8], F32)
            AG = self.sb(st, "AG", [128, 8, 24], F32)
            sinks = self.sb(st, "sinks", [128, self.LW * 8], F32)
            for name, t_ in (("cfar", cfar), ("share", share), ("selkeep", selkeep), ("seladd", seladd), ("mvalid", mvalid), ("madd", madd), ("mown", mown)):
                S.dma("sp", lambda q, t_=t_, name=name: q.dma_start(out=t_[:], in_=self.tb[name]), "ld_t_" + name, writes=[name])
            S.dma("sp", lambda q: q.dma_start(out=sinks[:], in_=self.sinks_in[:, :]), "ld_sinks", writes=["sinks"])
            S.dma("sp", lambda q: q.dma_start(out=AG[:], in_=self.AG_d.rearrange("(s p) n -> p s n", p=128)), "ld_AG", writes=["AG"])
            kcT = self.sb(st, "kcT", [128, 2, 128], BF16)
            vcc = self.sb(st, "vcc", [128, 2, 130], BF16)
            S.op("dve", lambda q: q.memset(kcT[:], 0.0), writes=[("kcT", 0), ("kcT", 1)])
            S.op("dve", lambda q: q.memset(vcc[:], 0.0), writes=[("vcc", 0), ("vcc", 1), "vcc1"])
            S.op("dve", lambda q: q.memset(vcc[:, :, 128:129], 1.0), writes=["vcc1"])
            self.compress(layer, kcT, vcc)
            KTr = [self.sb(st, f"KTr{i}", [128, 2048], BF16) for i in range(3)]
            Vr = [self.sb(st, f"Vr{i}", [128, 16, 130], BF16) for i in range(3)]
            for i in range(3):
                S.op("pool", lambda q, i=i: q.memset(Vr[i][:, :, 128:130], 1.0), writes=[("Vone", i)])
            QTg = [self.sb(st, f"QTg{i}", [128, 4, NT], BF16) for i in range(2)]
            NBr = [self.sb(st, f"NBr{i}", [128, 768], F32) for i in range(2)]
            BCr = [self.sb(st, f"BCr{i}", [128, 8, 128], F32) for i in range(2)]
            R = {"ac": 0,
                 "pend": [],
                 "L": [self.sb(st, f"L{i}", [128, 2048], F32) for i in range(NBUF)],
                 "Pb": [self.sb(st, f"Pb{i}", [128, 2048], BF16) for i in range(NBUF)],
                 "PT": [self.sb(st, f"PT{i}", [128, 2048], BF16) for i in range(NBUF)],
                 "sm": self.sb(st, "sm", [128, 8, 8], F32),
                 "mb": self.sb(st, "mb", [128, 8, 32], F32),
                 "P32": self.sb(st, "P32", [128, 512], F32),
                 "s4": self.sb(st, "s4", [128, 8, 16], F32),
                 "psum4": self.sb(st, "psum4", [128, 8, 128], F32)}
            OA = self.sb(st, "OA", [128, 8, 4, 128], F32)
            selm = self.sb(st, "selm", [128, 8, 32], F32)
            sc = self.sb(st, "sc", [128, 8, 32], F32)
            cm = [self.sb(st, "cm0", [128, 32, 32], F32)]
            cnt = self.sb(st, "cnt", [128, 8, 32], F32)
            p4Tall = self.sb(st, "p4T", [128, 8, 128], F32)
            kmf = self.sb(st, "kmf", [128, 8], F32)
            kmT = self.sb(st, "kmT", [128, 8], BF16)
            mm = [self.sb(st, f"mm{i}", [128, 8, 16], F32) for i in range(2)]
            msc = self.sb(st, "msc", [128, 64], F32)
            mcm = self.sb(st, "mcm", [128, 8, 8, 8], F32)
            mcnt = self.sb(st, "mcnt", [128, 8, 8], F32)
            OTs = [self.sb(st, f"OTs{i}", [128, 4, 128], BF16) for i in range(4)]
            rgb = self.sb(st, "rgb", [128, 8], F32)
            cnts = {"k": 0, "nb": 0, "bc": 0, "grp": 0, "ot": 0, "mm": 0}

            def load_kv(ktyp, vtyp, kvh):
                i = cnts["k"] % 3
                cnts["k"] += 1
                kt, v = KTr[i], Vr[i]
                ktv = kt[:].rearrange("p (s j r) -> p s j r", s=8, j=2)
                vv = v[:, :, 0:128].rearrange("p (s j) r -> p s j r", j=2)
                for j in range(2):
                    src = self.ktall(j, ktyp, kvh).rearrange("p (s r) -> p s r", r=128)
                    S.dma("sp", lambda q, ktv=ktv, j=j, src=src: q.dma_start(out=ktv[:, :, j, :], in_=src), f"ld_kt{i}", writes=[("KT", i)])
                    vsrc = self.V_all[j, vtyp].rearrange("(s p) n -> p s n", p=128)[:, :, kvh * 128:(kvh + 1) * 128]
                    S.dma("sp", lambda q, vv=vv, j=j, vsrc=vsrc: q.dma_start(out=vv[:, :, j, :], in_=vsrc), f"ld_v{i}", writes=[("V", i)])
                return kt, ("KT", i), v, ("V", i), ("Vone", i)

            def load_nb(name, h, width):
                i = cnts["nb"] % 2
                cnts["nb"] += 1
                S.dma("sp", lambda q, i=i: q.dma_start(out=NBr[i][:, 0:width], in_=self.tb[name][:, h, :]), f"ld_nb{i}", writes=[("NB", i)])
                return NBr[i], ("NB", i)

            for m in range(3):
                for kvh in range(2):
                    gi = cnts["grp"] % 2
                    cnts["grp"] += 1
                    QT = QTg[gi]
                    h0 = m * 8 + kvh * 4
                    S.dma("sp", lambda q, QT=QT, h0=h0: q.dma_start(out=QT[:], in_=self.QT_d[h0:h0 + 4].rearrange("h p l -> p h l")), f"ld_q{gi}",
                          writes=[("QT", gi)])
                    qkey = ("QT", gi)

                    def qt_ap(g, s, QT=QT):
                        return QT[:, g, s * 128:(s + 1) * 128]

                    def fin_first(s, g, gate):
                        def fin(bank_o, okey, sm, k8, smk):
                            oak = ("OA", s, g)
                            if gate is None:
                                S.op("dve", lambda q: q.tensor_scalar(out=OA[:, s, g, :], in0=bank_o[:, 0:128], scalar1=sm[:, k8, 3:4], scalar2=None, op0=ALU.mult),
                                     reads=[okey, smk + ("r",)], writes=[oak])
                            else:
                                S.op("dve", lambda q: q.tensor_tensor(out=sm[:, k8, 4:5], in0=sm[:, k8, 3:4], in1=gate, op=ALU.mult),
                                     reads=[smk + ("r",), "AG"], writes=[smk + ("rg",)])
                                S.op("dve", lambda q: q.tensor_scalar(out=OA[:, s, g, :], in0=bank_o[:, 0:128], scalar1=sm[:, k8, 4:5], scalar2=None, op0=ALU.mult),
                                     reads=[okey, smk + ("rg",)], writes=[oak])
                        return fin

                    def fin_acc(s, g, gate):
                        def fin(bank_o, okey, sm, k8, smk):
                            oak = ("OA", s, g)
                            S.op("dve", lambda q: q.tensor_tensor(out=sm[:, k8, 4:5], in0=sm[:, k8, 3:4], in1=gate, op=ALU.mult),
                                 reads=[smk + ("r",), "AG"], writes=[smk + ("rg",)])
                            S.op("dve", lambda q: q.scalar_tensor_tensor(out=OA[:, s, g, :], in0=bank_o[:, 0:128], scalar=sm[:, k8, 4:5], in1=OA[:, s, g, :],
                                                                         op0=ALU.mult, op1=ALU.add),
                                 reads=[okey, smk + ("rg",), oak], writes=[oak])
                        return fin

                    if m == 0:
                        for s in range(8):
                            bi = cnts["bc"] % 2
                            cnts["bc"] += 1
                            S.dma("sp", lambda q, bi=bi, s=s: q.dma_start(out=BCr[bi][:], in_=self.tb["bias_c"][s]), f"ld_bc{bi}", writes=[("BC", bi)])
                            self.attn_cmp4(R, QT, qkey, s, kvh, kcT, vcc, BCr[bi], ("BC", bi), AG, OA)
                        self.attn_bubble(R)
                        p4T = p4Tall
                        for half in range(2):
                            bank, bkey = self.ps()
                            for s4 in range(4):
                                s = half * 4 + s4
                                S.op("pe", lambda q, bank=bank, s=s, s4=s4: q.transpose(bank[:, s4 * 128:(s4 + 1) * 128], R["psum4"][:, s, :], self.identf[:]),
                                     reads=[("psum4", s), "identf"], writes=[bkey])
                            S.op("act", lambda q, bank=bank, half=half: q.activation(out=p4T[:, half * 4:(half + 1) * 4, :].rearrange("p s c -> p (s c)"), in_=bank[:], func=AF.Copy),
                                 reads=[bkey], writes=[("p4T", half)])
                        bank2, bkey2 = self.ps()
                        for s in range(8):
                            S.op("pe", lambda q, bank2=bank2, s=s: q.matmul(bank2[:, s * 32:(s + 1) * 32], p4T[:, s, :], share[:], start=True, stop=True),
                                 reads=[("p4T", s // 4), "share"], writes=[bkey2])
                        sc2 = sc[:].rearrange("p s b -> p (s b)")
                        S.op("dve", lambda q, bank2=bank2: q.tensor_tensor(out=sc2, in0=bank2[:, 0:256], in1=selkeep[:].rearrange("p s b -> p (s b)"), op=ALU.mult),
                             reads=[bkey2, "selkeep"], writes=["sc"])
                        S.op("dve", lambda q: q.tensor_tensor(out=sc2, in0=sc2, in1=seladd[:].rearrange("p s b -> p (s b)"), op=ALU.add), reads=["sc", "seladd"], writes=["sc"])
                        for s in range(8):
                            ci = 0
                            S.op("dve", lambda q, s=s, ci=ci: q.tensor_tensor(out=cm[ci][:], in0=sc[:, s, :].unsqueeze(1).broadcast_to([128, 32, 32]),
                                                                              in1=sc[:, s, :].unsqueeze(2).broadcast_to([128, 32, 32]), op=ALU.is_gt),
                                 reads=["sc"], writes=[("cm", ci)])
                            S.op("dve", lambda q, s=s, ci=ci: q.reduce_sum(out=cnt[:, s, :], in_=cm[ci][:], axis=AX.X), reads=[("cm", ci)], writes=[("cnt", s)])
                        S.op("dve", lambda q: q.tensor_scalar(out=selm[:].rearrange("p s b -> p (s b)"), in0=cnt[:].rearrange("p s b -> p (s b)"), scalar1=15.5, scalar2=NEG,
                                                              op0=ALU.is_gt, op1=ALU.mult),
                             reads=[("cnt", s) for s in range(8)], writes=[("selm", s) for s in range(8)])
                        kt, kkey, v, vkey, vone = load_kv(2, 0, kvh)
                        for g in range(4):
                            h = kvh * 4 + g
                            nb, nbkey = load_nb("nb_sel", h, 384)
                            for s in range(8):
                                ns = max(0, 2 * s - 1)
                                self.attn(R, qt_ap(g, s), qkey, kt, kkey, v, vkey, 0, 2 * s + 2, ns, nb, nbkey, (ns - (2 * s - 1)) * 128, cfar[:, h:h + 1],
                                          mask=selm[:, s, :], mkey=("selm", s), mblk=64, fin=fin_acc(s, g, AG[:, s, h * 3 + 1:h * 3 + 2]))
                        kt, kkey, v, vkey, vone = load_kv(3, 1, kvh)
                        for g in range(4):
                            h = kvh * 4 + g
                            nb, nbkey = load_nb("nb_win", h, 768)
                            for s in range(8):
                                k0 = max(0, 2 * s - 4)
                                self.attn(R, qt_ap(g, s), qkey, kt, kkey, v, vkey, k0, 2 * s + 2, k0, nb, nbkey, (k0 - (2 * s - 4)) * 128, None,
                                          fin=fin_acc(s, g, AG[:, s, h * 3 + 2:h * 3 + 3]))
                    elif m == 1:
                        kt, kkey, v, vkey, vone = load_kv(4, 2, kvh)
                        for g in range(4):
                            h = kvh * 4 + g
                            nb, nbkey = load_nb("nb_swa", h, 384)
                            sk = sinks[:, layer * 8 + h:layer * 8 + h + 1]
                            for s in range(8):
                                k0 = max(0, 2 * s - 1)
                                self.attn(R, qt_ap(g, s), qkey, kt, kkey, v, vkey, k0, 2 * s + 2, k0, nb, nbkey, (k0 - (2 * s - 1)) * 128, None,
                                          sink=sk, fin=fin_first(s, g, None))
                    else:
                        kt, kkey, v, vkey, vone = load_kv(5, 3, kvh)
                        S.op("dve", lambda q, kt=kt: q.reduce_sum(out=kmf[:], in_=kt[:].rearrange("p (b r) -> p b r", r=256), axis=AX.X), reads=[kkey], writes=["kmf"])
                        S.op("dve", lambda q: q.tensor_scalar(out=kmT[:], in0=kmf[:], scalar1=1.0 / 256.0, scalar2=None, op0=ALU.mult), reads=["kmf"], writes=["kmT"])
                        for g in range(4):
                            h = kvh * 4 + g
                            nb, nbkey = load_nb("nb_moba", h, 384)
                            hi = cnts["mm"] % 2
                            cnts["mm"] += 1
                            m16k = ("m16", hi)
                            bank, bkey = self.ps()
                            for s in range(8):
                                qa = qt_ap(g, s)
                                S.op("pe", lambda q, bank=bank, qa=qa, s=s: q.matmul(bank[:, s * 8:(s + 1) * 8], qa, kmT[:], start=True, stop=True),
                                     reads=[qkey, "kmT"], writes=[bkey])
                            mv64 = mvalid[:].rearrange("p s n -> p (s n)")
                            S.op("dve", lambda q, bank=bank, mv64=mv64: q.tensor_tensor(out=msc[:], in0=bank[:, 0:64], in1=mv64, op=ALU.mult), reads=[bkey, "mvalid"], writes=["msc"])
                            S.op("dve", lambda q: q.tensor_tensor(out=msc[:], in0=msc[:], in1=madd[:].rearrange("p s n -> p (s n)"), op=ALU.add), reads=["msc", "madd"], writes=["msc"])
                            ms3 = msc[:].rearrange("p (s n) -> p s n", n=8)
                            S.op("dve", lambda q, ms3=ms3: q.tensor_tensor(out=mcm[:], in0=ms3.unsqueeze(2).broadcast_to([128, 8, 8, 8]),
                                                                           in1=ms3.unsqueeze(3).broadcast_to([128, 8, 8, 8]), op=ALU.is_gt), reads=["msc"], writes=["mcm"])
                            S.op("dve", lambda q: q.reduce_sum(out=mcnt[:], in_=mcm[:], axis=AX.X), reads=["mcm"], writes=["mcnt"])
                            S.op("dve", lambda q: q.tensor_scalar(out=mcnt[:], in0=mcnt[:], scalar1=2.5, scalar2=None, op0=ALU.is_lt), reads=["mcnt"], writes=["mcnt"])
                            S.op("dve", lambda q: q.tensor_tensor(out=mcnt[:], in0=mcnt[:], in1=mvalid[:], op=ALU.mult), reads=["mcnt", "mvalid"], writes=["mcnt"])
                            S.op("dve", lambda q: q.tensor_tensor(out=mcnt[:], in0=mcnt[:], in1=mown[:], op=ALU.add), reads=["mcnt", "mown"], writes=["mcnt"])
                            S.op("dve", lambda q: q.tensor_scalar(out=mcnt[:], in0=mcnt[:], scalar1=1e30, scalar2=NEG, op0=ALU.mult, op1=ALU.add), reads=["mcnt"], writes=["mcnt"])
                            m16 = mm[hi]
                            S.op("dve", lambda q, m16=m16: q.tensor_copy(out=m16[:].rearrange("p s (n two) -> p (s n) two", two=2),
                                                                        in_=mcnt[:].rearrange("p s n -> p (s n)").unsqueeze(2).broadcast_to([128, 64, 2])),
                                 reads=["mcnt"], writes=[m16k])
                            if self.mode == "B":
                                S.dma("sp", lambda q, m16=m16, h=h: q.dma_start(out=self.dbg[h], in_=m16[:]), "st_dbg", reads=[m16k])
                            for s in range(8):
                                ns = max(0, 2 * s - 1)
                                self.attn(R, qt_ap(g, s), qkey, kt, kkey, v, vkey, 0, 2 * s + 2, ns, nb, nbkey, (ns - (2 * s - 1)) * 128,
                                          cfar[:, 16 + h:16 + h + 1], mask=m16[:, s, :], mkey=m16k, mblk=128, fin=fin_first(s, g, None))
                    def output_phase(h0=h0):
                        for s in range(8):
                            oi = cnts["ot"] % 4
                            cnts["ot"] += 1
                            bank, bkey = self.ps()
                            for g in range(4):
                                S.op("pe", lambda q, bank=bank, g=g, s=s: q.transpose(bank[:, g * 128:(g + 1) * 128], OA[:, s, g, :], self.identf[:]),
                                     reads=[("OA", s, g), "identf"], writes=[bkey])
                            S.op("act", lambda q, bank=bank, oi=oi: q.activation(out=OTs[oi][:].rearrange("p g r -> p (g r)"), in_=bank[:], func=AF.Copy),
                                 reads=[bkey], writes=[("OTs", oi)])
                            dst = self.OT_d[h0:h0 + 4, :, s * 128:(s + 1) * 128].rearrange("h p l -> p h l")
                            S.dma("sp", lambda q, oi=oi, dst=dst: q.dma_start(out=dst, in_=OTs[oi][:]), f"st_ot{oi}", reads=[("OTs", oi)])

                    self.attn_marker(R, output_phase)
            self.attn_drain(R)
            S.barrier()
            S.flush()

    def phase_C(self, layer):
        S = self.S
        with ExitStack() as st:
            OTh = self.sb(st, "OTh", [128, 24, 512], BF16)
            ZT = self.sb(st, "ZT", [128, 16, 512], BF16)
            Gs = [self.sb(st, f"Gs{i}", [128, 512], F32) for i in range(6)]
            WB = [self.sb(st, f"WB{i}", [128, 8, 512], BF16) for i in range(6)]
            WO = [self.sb(st, f"WO{i}", [128, 16, 512], BF16) for i in range(2)]
            zacc = [self.sb(st, f"zacc{i}", [128, 512], F32) for i in range(2)]
            tmp = [self.sb(st, f"ztmp{i}", [128, 512], F32) for i in range(2)]
            cn = {"g": 0, "wb": 0, "wo": 0, "z": 0, "t": 0}
            steps = []
            for half in range(2):
                cols = slice(half * 512, (half + 1) * 512)

                def ld_ot(half=half, cols=cols):
                    for m in range(3):
                        S.dma("sp", lambda q, m=m: q.dma_start(out=OTh[:, m * 8:(m + 1) * 8, :], in_=self.OT_d[m * 8:(m + 1) * 8, :, cols].rearrange("h p l -> p h l")),
                              f"ld_oth{m}", writes=[("OTh", m)])

                for fg in range(4):
                    box = {}

                    def ld(fg=fg, box=box, half=half, cols=cols, ld_ot=ld_ot):
                        if fg == 0:
                            ld_ot()
                        box["wb"] = []
                        for m in range(3):
                            wi = cn["wb"] % 6
                            cn["wb"] += 1
                            src = self.w_branch[layer, m, :, fg * 512:(fg + 1) * 512].rearrange("(h p) n -> p h n", p=128)
                            S.dma("pool", lambda q, wi=wi, src=src: q.dma_start(out=WB[wi][:], in_=src), f"ld_wb{wi}", writes=[("WB", wi)])
                            box["wb"].append(wi)

                    def comp(fg=fg, box=box, half=half, cols=cols):
                        for fi in range(4):
                            f = fg * 4 + fi
                            zi = cn["z"] % 2
                            cn["z"] += 1
                            for m in range(3):
                                gi = cn["g"] % 6
                                cn["g"] += 1
                                S.dma("sp", lambda q, gi=gi, m=m, f=f: q.dma_start(out=Gs[gi][:], in_=self.G_d[m * 16 + f, :, cols]), f"ld_g{gi}", writes=[("Gs", gi)])
                                wi = box["wb"][m]
                                bank, bkey = self.ps()
                                for h in range(8):
                                    S.op("pe", lambda q, bank=bank, wi=wi, h=h, fi=fi, m=m: q.matmul(bank[:], WB[wi][:, h, fi * 128:(fi + 1) * 128], OTh[:, m * 8 + h, :],
                                                                                                 start=(h == 0), stop=(h == 7)),
                                         reads=[("WB", wi), ("OTh", m)], writes=[bkey])
                                if m == 0:
                                    S.op("dve", lambda q, bank=bank, zi=zi, gi=gi: q.tensor_tensor(out=zacc[zi][:], in0=bank[:], in1=Gs[gi][:], op=ALU.mult),
                                         reads=[bkey, ("Gs", gi)], writes=[("zacc", zi)])
                                else:
                                    ti = cn["t"] % 2
                                    cn["t"] += 1
                                    S.op("dve", lambda q, bank=bank, ti=ti, gi=gi: q.tensor_tensor(out=tmp[ti][:], in0=bank[:], in1=Gs[gi][:], op=ALU.mult),
                                         reads=[bkey, ("Gs", gi)], writes=[("ztmp", ti)])
                                    if m == 1:
                                        S.op("pool", lambda q, zi=zi, ti=ti: q.tensor_tensor(out=zacc[zi][:], in0=zacc[zi][:], in1=tmp[ti][:], op=ALU.add),
                                             reads=[("zacc", zi), ("ztmp", ti)], writes=[("zacc", zi)])
                                    else:
                                        S.op("pool", lambda q, zi=zi, ti=ti, f=f: q.tensor_tensor(out=ZT[:, f, :], in0=zacc[zi][:], in1=tmp[ti][:], op=ALU.add),
                                             reads=[("zacc", zi), ("ztmp", ti)], writes=[("ZT", f)])

                    steps.append((ld, comp))
                for fo in range(4):
                    box = {}

                    def ld(fo=fo, box=box):
                        wi = cn["wo"] % 2
                        cn["wo"] += 1
                        src = self.w_out[layer, :, fo * 512:(fo + 1) * 512].rearrange("(k p) n -> p k n", p=128)
                        for kh in range(2):
                            S.dma("pool", lambda q, wi=wi, src=src, kh=kh: q.dma_start(out=WO[wi][:, kh * 8:(kh + 1) * 8, :], in_=src[:, kh * 8:(kh + 1) * 8, :]),
                                  f"ld_wo{wi}", writes=[("WO", wi)])
                        box["wi"] = wi

                    def comp(fo=fo, box=box, cols=cols):
                        wi = box["wi"]
                        for fi in range(4):
                            f2 = fo * 4 + fi
                            bank, bkey = self.ps()
                            for k in range(16):
                                S.op("pe", lambda q, bank=bank, wi=wi, k=k, fi=fi: q.matmul(bank[:], WO[wi][:, k, fi * 128:(fi + 1) * 128], ZT[:, k, :], start=(k == 0), stop=(k == 15)),
                                     reads=[("WO", wi), ("ZT", k)], writes=[bkey])
                            S.op("dve", lambda q, bank=bank, f2=f2: q.tensor_tensor(out=self.XT[:, f2, cols], in0=bank[:], in1=self.XT[:, f2, cols], op=ALU.add),
                                 reads=[bkey, ("XT", f2)], writes=[("XT", f2)])

                    steps.append((ld, comp))
            steps[0][0]()
            for i, (ld, comp) in enumerate(steps):
                if i + 1 < len(steps):
                    steps[i + 1][0]()
                comp()
            S.barrier()
            S.flush()

    def phase_D(self, layer):
        S = self.S
        with ExitStack() as st:
            H2 = self.sb(st, "H2", [128, 16, NT], BF16)
            self.rmsnorm(st, layer * 2 + 1, H2, "H2")
            W1r = [self.sb(st, f"W1r{i}", [128, 16, 512], BF16) for i in range(2)]
            W2r = [self.sb(st, f"W2r{i}", [128, 4, D], BF16) for i in range(2)]
            U = [self.sb(st, f"U{i}", [128, 4, NT], BF16) for i in range(2)]
            sq = [self.sb(st, f"fsq{i}", [128, 512], F32) for i in range(2)]
            cn = {"sq": 0}

            def ld(g):
                wi = g % 2
                src1 = self.w_mlp_in[layer, :, g * 512:(g + 1) * 512].rearrange("(k p) n -> p k n", p=128)
                for kh in range(2):
                    S.dma("pool", lambda q, wi=wi, src1=src1, kh=kh: q.dma_start(out=W1r[wi][:, kh * 8:(kh + 1) * 8, :], in_=src1[:, kh * 8:(kh + 1) * 8, :]),
                          f"ld_w1{wi}", writes=[("W1r", wi)])
                src2 = self.w_mlp_out[layer, g * 512:(g + 1) * 512, :].rearrange("(c p) n -> p c n", p=128)
                for ch in range(2):
                    S.dma("pool", lambda q, wi=wi, src2=src2, ch=ch: q.dma_start(out=W2r[wi][:, ch * 2:(ch + 1) * 2, :], in_=src2[:, ch * 2:(ch + 1) * 2, :]),
                          f"ld_w2{wi}", writes=[("W2r", wi)])

            ld(0)
            for g in range(16):
                if g + 1 < 16:
                    ld(g + 1)
                wi = g % 2
                ui = g % 2
                for hc in range(4):
                    for half in range(2):
                        cols = slice(half * 512, (half + 1) * 512)
                        bank, bkey = self.ps()
                        for k in range(16):
                            S.op("pe", lambda q, bank=bank, wi=wi, k=k, hc=hc, cols=cols: q.matmul(bank[:], W1r[wi][:, k, hc * 128:(hc + 1) * 128], H2[:, k, cols],
                                                                                            start=(k == 0), stop=(k == 15)),
                                 reads=[("W1r", wi), ("H2", k)], writes=[bkey])
                        si = cn["sq"] % 2
                        cn["sq"] += 1
                        S.op("act", lambda q, bank=bank, si=si: q.activation(out=sq[si][:], in_=bank[:], func=AF.Square), reads=[bkey], writes=[("fsq", si)])
                        S.op("dve", lambda q, bank=bank, si=si, ui=ui, hc=hc, cols=cols: q.scalar_tensor_tensor(out=U[ui][:, hc, cols], in0=bank[:], scalar=0.0, in1=sq[si][:],
                                                                                                           op0=ALU.is_gt, op1=ALU.mult),
                             reads=[bkey, ("fsq", si)], writes=[("U", ui, hc, half)])
                for fo in range(16):
                    for half in range(2):
                        cols = slice(half * 512, (half + 1) * 512)
                        bank, bkey = self.ps()
                        for hc in range(4):
                            S.op("pe", lambda q, bank=bank, wi=wi, hc=hc, fo=fo, ui=ui, cols=cols: q.matmul(bank[:], W2r[wi][:, hc, fo * 128:(fo + 1) * 128], U[ui][:, hc, cols],
                                                                                                    start=(hc == 0), stop=(hc == 3)),
                                 reads=[("W2r", wi), ("U", ui, hc, half)], writes=[bkey])
                        S.op("dve", lambda q, bank=bank, fo=fo, cols=cols: q.tensor_tensor(out=self.XT[:, fo, cols], in0=bank[:], in1=self.XT[:, fo, cols], op=ALU.add),
                             reads=[bkey, ("XT", fo)], writes=[("XT", fo)])
            S.barrier()
            S.flush()

    def final_norm_store(self, do_norm):
        S = self.S
        with ExitStack() as st:
            if do_norm:
                self.rmsnorm(st, 2 * self.LW, None, "XTn", out_f32=self.XT)
            xo = self.xT_out.rearrange("(c p) l -> p c l", p=128)
            for c4 in range(4):
                rk = [("XTn", c) for c in range(c4 * 4, c4 * 4 + 4)] + [("XT", c) for c in range(c4 * 4, c4 * 4 + 4)]
                S.dma("sp", lambda q, c4=c4: q.dma_start(out=xo[:, c4 * 4:(c4 + 1) * 4, :], in_=self.XT[:, c4 * 4:(c4 + 1) * 4, :]), "st_x", reads=rk)
            S.barrier()
            S.flush()

    def exchange(self, kvkeys):
        S = self.S
        rg = [[0, 1], [2, 3], [4, 5], [6, 7]]
        for g in range(2):
            S.dma("pool", lambda q, g=g: q.collective_compute("AllGather", ALU.bypass, replica_groups=rg, ins=[self.KT_loc2[g]], outs=[self.KT_all2[g]]),
                  f"cc_k{g}", reads=kvkeys, writes=["KT_all"], inc=1)
        S.dma("pool", lambda q: q.collective_compute("AllGather", ALU.bypass, replica_groups=rg, ins=[self.V_loc2], outs=[self.V_all2]),
              "cc_v", reads=kvkeys, writes=["V_all"], inc=1)

    def build(self):
        self.declare()
        self.persistent()
        for li, layer in enumerate(self.layers):
            if self.mode in ("A", "fused"):
                self.phase_A(li)
            if self.mode in ("B", "BCD", "fused"):
                self.phase_B(li)
            if self.mode in ("BCD", "fused"):
                self.phase_C(li)
                self.phase_D(li)
        if self.mode in ("BCD", "fused"):
            self.final_norm_store(self.final_norm)
        self.S.barrier()
        self.S.flush()
        self.top.close()
        return self.nc

FUSED = True
_PROGS = {}


def _prog(mode, nlayers, final):
    key = (mode, nlayers, final)
    if key not in _PROGS:
        P = Prog(mode, list(range(nlayers)), final)
        _PROGS[key] = P.build()
    return _PROGS[key]


def _gam_arr(gs):
    return np.ascontiguousarray(np.concatenate([np.asarray(g, np.float32).reshape(16, 128).T for g in gs], axis=1))


def _own(a, j):
    return a.reshape(8, 2, 128, -1)[:, j].reshape(1024, -1)


def kernel(x, w_in, cmp_pos, cmp_w1, cmp_w2, swa_sinks, w_branch, w_out, w_mlp_in, w_mlp_out,
           norm_mix, norm_mlp, norm_final, rel_bias):
    f32 = lambda a: np.ascontiguousarray(np.asarray(a, dtype=np.float32))
    x = f32(x); w_in = f32(w_in); cmp_pos = f32(cmp_pos); cmp_w1 = f32(cmp_w1); cmp_w2 = f32(cmp_w2)
    swa_sinks = f32(swa_sinks); w_branch = f32(w_branch); w_out = f32(w_out); w_mlp_in = f32(w_mlp_in)
    w_mlp_out = f32(w_mlp_out); norm_mix = f32(norm_mix); norm_mlp = f32(norm_mlp); norm_final = f32(norm_final)
    rel_bias = f32(rel_bias)
    consts = host_consts()
    tables = [host_tables(rel_bias, j) for j in range(2)]
    posT_all = np.ascontiguousarray(np.transpose(cmp_pos, (0, 1, 3, 2)))
    cores = list(range(8))
    xT = [np.ascontiguousarray(_own(x[c // 2], c % 2).T) for c in cores]

    if FUSED:
        gam = _gam_arr([g for l in range(DEPTH) for g in (norm_mix[l], norm_mlp[l])] + [norm_final])
        sinks = np.ascontiguousarray(np.broadcast_to(swa_sinks.reshape(1, DEPTH * 8), (128, DEPTH * 8)))
        nc = _prog("fused", DEPTH, True)
        ins = []
        for c in cores:
            m = {"xT": xT[c], "w_in": w_in, "posT": posT_all, "cmp_w1": cmp_w1, "cmp_w2": cmp_w2, "w_branch": w_branch, "w_out": w_out,
                 "w_mlp_in": w_mlp_in, "w_mlp_out": w_mlp_out, "gam": gam, "sinks": sinks}
            m.update(tables[c % 2]); m.update(consts)
            ins.append(m)
        res = run_bass_kernel_spmd(nc, ins, core_ids=cores)
        xT = [res.results[c]["xT_out"] for c in cores]
    else:
        for l in range(DEPTH):
            gam = _gam_arr([norm_mix[l], norm_mlp[l], norm_final])
            sinks = np.ascontiguousarray(np.broadcast_to(swa_sinks[l][None], (128, 8)))
            base = []
            for c in cores:
                m = {"xT": xT[c], "gam": gam, "sinks": sinks}
                m.update(tables[c % 2]); m.update(consts)
                base.append(m)
            ncA = _prog("A", 1, False)
            insA = [dict(base[c], w_in=w_in[l:l + 1]) for c in cores]
            rA = run_bass_kernel_spmd(ncA, insA, core_ids=cores).results
            ncB = _prog("BCD", 1, l == DEPTH - 1)
            insB = []
            for c in cores:
                c0 = (c // 2) * 2
                m = dict(base[c], QT_d=rA[c]["QT_d"], AG_d=rA[c]["AG_d"], G_d=rA[c]["G_d"],
                         KT_all=np.stack([rA[c0]["KT_loc"], rA[c0 + 1]["KT_loc"]]), V_all=np.stack([rA[c0]["V_loc"], rA[c0 + 1]["V_loc"]]),
                         posT=posT_all[l:l + 1], cmp_w1=cmp_w1[l:l + 1], cmp_w2=cmp_w2[l:l + 1], w_branch=w_branch[l:l + 1],
                         w_out=w_out[l:l + 1], w_mlp_in=w_mlp_in[l:l + 1], w_mlp_out=w_mlp_out[l:l + 1])
                insB.append(m)
            rB = run_bass_kernel_spmd(ncB, insB, core_ids=cores).results
            xT = [rB[c]["xT_out"] for c in cores]
    out = np.zeros((4, 2048, D), np.float32)
    for c in cores:
        b, j = c // 2, c % 2
        out[b].reshape(8, 2, 128, D)[:, j] = np.asarray(xT[c], np.float32).T.reshape(8, 128, D)
    return out
```

```python
import math
from contextlib import ExitStack

import numpy as np
import concourse.bass as bass
import concourse.mybir as mybir
from concourse.bass_utils import run_bass_kernel_spmd

F32 = mybir.dt.float32
BF16 = mybir.dt.bfloat16
AF = mybir.ActivationFunctionType
ALU = mybir.AluOpType
AX = mybir.AxisListType

DEPTH = 4
D = 2048
DIN = 11800
DFF = 8192
NT = 1024
SCALE = 128 ** -0.5
NEG = -1e30
NBUF = 2


class Sched:
    ENGS = ("pe", "dve", "act", "pool", "sp")
    HANDLES = {"pe": "tensor", "dve": "vector", "act": "scalar", "pool": "gpsimd", "sp": "sync"}

    def __init__(self, nc, stack):
        self.nc = nc
        self.stack = stack
        self.prog = {e: [] for e in self.ENGS}
        self.cnt = {}
        self.semh = {}
        self.seen = {e: {} for e in self.ENGS}
        self.last_w = {}
        self.readers = {}
        self.n_ops = 0

    def _sem(self, name):
        if name not in self.semh:
            self.semh[name] = self.stack.enter_context(self.nc.semaphore(name))
            self.cnt[name] = 0
        return name

    def _deps(self, eng, group, reads, writes):
        need = {}

        def add(tok):
            if tok[1] > need.get(tok[0], 0):
                need[tok[0]] = tok[1]

        for k in reads:
            w = self.last_w.get(k)
            if w is not None:
                add(w)
        for k in writes:
            w = self.last_w.get(k)
            if w is not None and w[2] != group:
                add(w)
            for r in self.readers.get(k, ()):
                if r[2] != group:
                    add(r)
        waits = []
        seen = self.seen[eng]
        for sem, val in need.items():
            if seen.get(sem, 0) < val:
                seen[sem] = val
                waits.append((sem, val))
        return waits

    def _commit(self, tok, reads, writes):
        for k in writes:
            self.last_w[k] = tok
            self.readers[k] = []
        for k in reads:
            self.readers.setdefault(k, []).append(tok)

    def op(self, eng, fn, reads=(), writes=()):
        reads = tuple(reads); writes = tuple(writes)
        waits = self._deps(eng, eng, reads, writes)
        sem = self._sem("E_" + eng)
        self.cnt[sem] += 1
        self._commit((sem, self.cnt[sem], eng), reads, writes)
        self.prog[eng].append((waits, fn, sem, 1))
        self.n_ops += 1

    def dma(self, eng, fn, sem, reads=(), writes=(), inc=16):
        reads = tuple(reads); writes = tuple(writes)
        sem = self._sem("D_" + sem)
        waits = self._deps(eng, sem, reads, writes)
        self.cnt[sem] += inc
        self._commit((sem, self.cnt[sem], sem), reads, writes)
        self.prog[eng].append((waits, fn, sem, inc))
        self.n_ops += 1

    def barrier(self):
        for eng in self.ENGS:
            waits = []
            for sem, val in self.cnt.items():
                if val > 0 and self.seen[eng].get(sem, 0) < val:
                    self.seen[eng][sem] = val
                    waits.append((sem, val))
            if waits:
                self.prog[eng].append((waits, None, None, 0))
        self.last_w = {}
        self.readers = {}

    def flush(self):
        nc = self.nc
        semh = self.semh
        if not any(self.prog.values()):
            return
        with nc.Block() as block:
            for eng in self.ENGS:
                prog = self.prog[eng]
                if not prog:
                    continue

                def body(e, prog=prog):
                    for waits, fn, sem, inc in prog:
                        for s, v in waits:
                            e.wait_ge(semh[s], v)
                        if fn is not None:
                            fn(e).then_inc(semh[sem], inc)

                getattr(block, self.HANDLES[eng])(body)
        self.prog = {e: [] for e in self.ENGS}


def _bucket(dist):
    n = np.maximum(dist, 0)
    lr = np.log(np.maximum(n, 1).astype(np.float32) / np.float32(16)) / np.float32(math.log(8.0))
    large = np.minimum(16 + (lr * np.float32(16)).astype(np.int32), 31)
    return np.where(n < 16, n, large).astype(np.int64)


def _bias_tbl(tbl_ext, heads, dist, masked):
    idx = np.where(masked, 32, _bucket(dist))
    out = tbl_ext[idx][:, :, heads]
    return np.ascontiguousarray(np.transpose(out, (0, 2, 1)))


def host_tables(rel_bias, j):
    tbl_ext = np.concatenate([rel_bias.astype(np.float32), np.full((1, 24), NEG, np.float32)], axis=0)
    r = np.arange(128)[:, None]
    t = {}
    kk = np.arange(384)[None, :]
    dist = (j + 1) * 128 + r - kk
    t["nb_sel"] = _bias_tbl(tbl_ext, np.arange(0, 8), dist, dist < 0)
    t["nb_moba"] = _bias_tbl(tbl_ext, np.arange(16, 24), dist, dist < 0)
    t["nb_swa"] = _bias_tbl(tbl_ext, np.arange(8, 16), dist, (dist < 0) | (dist >= 128))
    kk = np.arange(768)[None, :]
    dist = (j + 4) * 128 + r - kk
    t["nb_win"] = _bias_tbl(tbl_ext, np.arange(0, 8), dist, (dist < 0) | (dist >= 512))
    t["cfar"] = np.ascontiguousarray(np.broadcast_to(tbl_ext[31][None, :], (128, 24)))
    bc = np.zeros((8, 128, 8, 128), np.float32)
    c = np.arange(128)[None, :]
    for s in range(8):
        dist = (2 * s + j) * 128 + r - 16 * c - 31
        bc[s] = _bias_tbl(tbl_ext, np.arange(0, 8), dist, (dist < 0) | (c >= 127))
    t["bias_c"] = bc
    keep = np.zeros((128, 8, 32), np.float32)
    add = np.zeros((128, 8, 32), np.float32)
    blk = np.arange(32)[None, :]
    for s in range(8):
        cur = 4 * s + 2 * j + (np.arange(128)[:, None] >= 64)
        a = np.zeros((128, 32), np.float32)
        a = np.where(blk > cur, np.float32(-1e30), a)
        a = np.where(blk == cur - 1, np.float32(1e30), a)
        a = np.where(blk == cur, np.float32(2e30), a)
        a = np.where(blk == 0, np.float32(3e30), a)
        add[:, s, :] = a
        keep[:, s, :] = (a == 0)
    t["selkeep"] = keep
    t["seladd"] = add
    return t


def host_consts():
    c = {}
    starts = np.arange(127)[:, None] * 16
    blk = np.arange(32)[None, :] * 64
    shared = np.clip(np.minimum(starts + 32, blk + 64) - np.maximum(starts, blk), 0, None)
    sh = np.zeros((128, 32), np.float32)
    sh[:127] = (shared / 16).astype(np.float32)
    c["share"] = sh
    c["ident"] = np.eye(128, dtype=np.float32)
    n = np.arange(8)[None, :]
    s = np.arange(8)[:, None]
    mv = (n < s).astype(np.float32)
    c["mvalid"] = np.ascontiguousarray(np.broadcast_to(mv[None], (128, 8, 8)))
    c["madd"] = np.ascontiguousarray(np.broadcast_to(np.where(n < s, 0.0, NEG).astype(np.float32)[None], (128, 8, 8)))
    c["mown"] = np.ascontiguousarray(np.broadcast_to((n == s).astype(np.float32)[None], (128, 8, 8)))
    return c


class Prog:
    def __init__(self, mode, layers, final_norm):
        self.mode = mode
        self.layers = layers
        self.LW = len(layers)
        self.final_norm = final_norm
        self.nc = bass.Bass("TRN2", target_bir_lowering=False)
        self.top = ExitStack()
        self.S = Sched(self.nc, self.top)
        self.psn = 0
        self.uid = 0

    def dram(self, name, shape, dt, kind="Internal"):
        return self.nc.dram_tensor(name, list(shape), dt, kind=kind).ap()

    def sb(self, st, name, shape, dt):
        self.uid += 1
        return st.enter_context(self.nc.sbuf_tensor(f"{name}_{self.uid}", list(shape), dt))

    def ktloc(self, t, k):
        if isinstance(self.KT_loc, list):
            return self.KT_loc[t // 3][t % 3, k]
        return self.KT_loc[t, k]

    def ktall(self, j, t, k):
        if isinstance(self.KT_all, list):
            return self.KT_all[t // 3][j, t % 3, k]
        return self.KT_all[j, t, k]

    def ps(self):
        i = self.psn % 8
        self.psn += 1
        return self.psb[i], ("ps", i)

    def declare(self):
        nc = self.nc
        ext_in = "ExternalInput"
        self.xT_in = self.dram("xT", [D, NT], F32, ext_in)
        mode = self.mode
        hasA = mode in ("A", "fused")
        hasB = mode in ("B", "BCD", "fused")
        hasCD = mode in ("BCD", "fused")
        if hasA:
            self.w_in = self.dram("w_in", [self.LW, D, DIN], F32, ext_in)
        if hasB:
            self.posT = self.dram("posT", [self.LW, 2, 128, 32], F32, ext_in)
            self.cmp_w1 = self.dram("cmp_w1", [self.LW, 2, 4096, 256], F32, ext_in)
            self.cmp_w2 = self.dram("cmp_w2", [self.LW, 2, 256, 128], F32, ext_in)
        if hasCD:
            self.w_branch = self.dram("w_branch", [self.LW, 3, 1024, D], F32, ext_in)
            self.w_out = self.dram("w_out", [self.LW, D, D], F32, ext_in)
            self.w_mlp_in = self.dram("w_mlp_in", [self.LW, D, DFF], F32, ext_in)
            self.w_mlp_out = self.dram("w_mlp_out", [self.LW, DFF, D], F32, ext_in)
        self.gam_in = self.dram("gam", [128, (2 * self.LW + 1) * 16], F32, ext_in)
        self.sinks_in = self.dram("sinks", [128, self.LW * 8], F32, ext_in)
        self.tb = {}
        for name, shape in (("nb_sel", [128, 8, 384]), ("nb_moba", [128, 8, 384]), ("nb_swa", [128, 8, 384]),
                            ("nb_win", [128, 8, 768]), ("cfar", [128, 24]), ("bias_c", [8, 128, 8, 128]),
                            ("selkeep", [128, 8, 32]), ("seladd", [128, 8, 32]), ("share", [128, 32]),
                            ("ident", [128, 128]), ("mvalid", [128, 8, 8]), ("madd", [128, 8, 8]),
                            ("mown", [128, 8, 8])):
            self.tb[name] = self.dram(name, shape, F32, ext_in)
        fused = self.mode == "fused"
        kA = "Internal" if fused else ("ExternalOutput" if self.mode == "A" else "ExternalInput")
        self.QT_d = self.dram("QT_d", [24, 128, NT], BF16, kA)
        self.AG_d = self.dram("AG_d", [NT, 24], F32, kA)
        if mode != "B":
            self.G_d = self.dram("G_d", [48, 128, NT], F32, kA)
        if self.mode in ("A", "fused"):
            if fused:
                self.KT_loc2 = [self.dram(f"KT_loc{g}", [3 * 2 * 128, NT], BF16, "Internal") for g in range(2)]
                self.V_loc2 = self.dram("V_loc", [4 * NT, 256], BF16, "Internal")
                self.KT_loc = [a.rearrange("(t k p) l -> t k p l", t=3, k=2) for a in self.KT_loc2]
                self.V_loc = self.V_loc2.rearrange("(t l) n -> t l n", t=4)
            else:
                self.KT_loc = self.dram("KT_loc", [6, 2, 128, NT], BF16, "ExternalOutput")
                self.V_loc = self.dram("V_loc", [4, NT, 256], BF16, "ExternalOutput")
        if hasB:
            if fused:
                self.KT_all2 = [self.dram(f"KT_all{g}", [2 * 3 * 2 * 128, NT], BF16, "Internal") for g in range(2)]
                self.V_all2 = self.dram("V_all", [2 * 4 * NT, 256], BF16, "Internal")
                self.KT_all = [a.rearrange("(j t k p) l -> j t k p l", j=2, t=3, k=2) for a in self.KT_all2]
                self.V_all = self.V_all2.rearrange("(j t l) n -> j t l n", j=2, t=4)
            else:
                self.KT_all = self.dram("KT_all", [2, 6, 2, 128, NT], BF16, "ExternalInput")
                self.V_all = self.dram("V_all", [2, 4, NT, 256], BF16, "ExternalInput")
            self.OT_d = self.dram("OT_d", [24, 128, NT], BF16, "ExternalOutput" if mode == "B" else "Internal")
            if mode == "B":
                self.dbg = self.dram("dbg", [8, 128, 8, 16], F32, "ExternalOutput")
                self.dbg2 = self.dram("dbg2", [2, 128, 8], F32, "ExternalOutput")
        if hasCD:
            self.xT_out = self.dram("xT_out", [D, NT], F32, "ExternalOutput")

    def persistent(self):
        st = self.top
        nc = self.nc
        S = self.S
        self.XT = self.sb(st, "XT", [128, 16, NT], F32)
        self.gam = self.sb(st, "gam", [128, (2 * self.LW + 1) * 16], F32)
        self.identf = self.sb(st, "identf", [128, 128], F32)
        self.identb = self.sb(st, "identb", [128, 128], BF16)
        self.onesf = self.sb(st, "onesf", [128, 128], F32)
        self.psb = [st.enter_context(nc.psum_tensor(f"psb{i}", [128, 512], F32)) for i in range(8)]
        xin = self.xT_in.rearrange("(c p) l -> p c l", p=128)
        for c4 in range(4):
            S.dma("sp", lambda q, c4=c4: q.dma_start(out=self.XT[:, c4 * 4:(c4 + 1) * 4, :], in_=xin[:, c4 * 4:(c4 + 1) * 4, :]),
                  f"ld_x{c4}", writes=[("XT", c) for c in range(c4 * 4, c4 * 4 + 4)])
        S.dma("sp", lambda q: q.dma_start(out=self.gam[:], in_=self.gam_in[:, :]), "ld_gam", writes=["gam"])
        S.dma("sp", lambda q: q.dma_start(out=self.identf[:], in_=self.tb["ident"][:, :]), "ld_idf", writes=["identf"])
        S.dma("pool", lambda q: q.dma_start(out=self.identb[:], in_=self.tb["ident"][:, :]), "ld_c2", writes=["identb"])
        S.op("dve", lambda q: q.memset(self.onesf[:], 1.0), writes=["onesf"])

    def rmsnorm(self, st, nidx, HT, htkey, out_f32=None):
        S = self.S
        RS = self.sb(st, "RS", [128, NT], F32)
        SQ = [self.sb(st, f"SQ{i}", [128, 512], F32) for i in range(2)]
        for half in range(2):
            cols = slice(half * 512, (half + 1) * 512)
            bank, bkey = self.ps()
            for c in range(16):
                sq = SQ[c % 2]
                sqk = ("SQ", c % 2)
                S.op("act", lambda q, sq=sq, c=c, cols=cols: q.activation(out=sq[:], in_=self.XT[:, c, cols], func=AF.Square),
                     reads=[("XT", c)], writes=[sqk])
                S.op("pe", lambda q, sq=sq, c=c, bank=bank: q.matmul(bank[:], self.onesf[:], sq[:], start=(c == 0), stop=(c == 15)),
                     reads=[sqk, "onesf"], writes=[bkey])
            S.op("dve", lambda q, bank=bank, cols=cols: q.tensor_scalar(out=RS[:, cols], in0=bank[:], scalar1=1.0 / D, scalar2=1e-6,
                                                                       op0=ALU.mult, op1=ALU.add),
                 reads=[bkey], writes=[("RS", half)])
            S.op("act", lambda q, cols=cols: q.activation(out=RS[:, cols], in_=RS[:, cols], func=AF.Sqrt),
                 reads=[("RS", half)], writes=[("RS", half)])
            S.op("dve", lambda q, cols=cols: q.reciprocal(out=RS[:, cols], in_=RS[:, cols]),
                 reads=[("RS", half)], writes=[("RS", half)])
        for c in range(16):
            g = self.gam[:, nidx * 16 + c:nidx * 16 + c + 1]
            dst = HT[:, c, :] if out_f32 is None else out_f32[:, c, :]
            S.op("dve", lambda q, c=c, g=g, dst=dst: q.scalar_tensor_tensor(out=dst, in0=self.XT[:, c, :], scalar=g, in1=RS[:],
                                                                            op0=ALU.mult, op1=ALU.mult),
                 reads=[("XT", c), ("RS", 0), ("RS", 1), "gam"], writes=[(htkey, c)])

    def phase_A(self, layer):
        S = self.S
        nc = self.nc
        with ExitStack() as st:
            HT = self.sb(st, "HT", [128, 16, NT], BF16)
            self.rmsnorm(st, layer * 2, HT, "HT")
            htr = [("HT", c) for c in range(16)]
            NSL = 4
            slots = [self.sb(st, f"wsl{i}", [128, 16, 512], BF16) for i in range(NSL)]
            FMo = [self.sb(st, f"fmo{i}", [128, NT], BF16) for i in range(3)]
            GTo = [self.sb(st, f"gto{i}", [128, NT], F32) for i in range(3)]
            TMo = [self.sb(st, f"tmo{i}", [128, 8, 256], BF16) for i in range(2)]
            AGo = self.sb(st, "ago", [128, 8, 24], F32)
            blocks = []

            def fm(sub, dst, sig=False):
                return ("fm", sub, dst, sig)

            blocks.append((0, 512, [fm(u * 128, self.QT_d[u]) for u in range(4)]))
            blocks.append((512, 512, [fm(u * 128, self.QT_d[4 + u]) for u in range(4)]))
            blocks.append((1024, 512, [fm(0, self.ktloc(0, 0)), fm(128, self.ktloc(0, 1)),
                                       fm(256, self.ktloc(1, 0)), fm(384, self.ktloc(1, 1))]))
            blocks.append((1536, 512, [fm(0, self.ktloc(2, 0)), fm(128, self.ktloc(2, 1)), ("tm", 256, 256, self.V_loc[0], False)]))
            blocks.append((2048, 512, [fm(0, self.ktloc(3, 0)), fm(128, self.ktloc(3, 1)), ("tm", 256, 256, self.V_loc[1], False)]))
            blocks.append((2560, 24, [("tm", 0, 24, self.AG_d, True)]))
            blocks.append((2584, 512, [fm(u * 128, self.QT_d[8 + u]) for u in range(4)]))
            blocks.append((3096, 512, [fm(u * 128, self.QT_d[12 + u]) for u in range(4)]))
            blocks.append((3608, 512, [fm(0, self.ktloc(4, 0)), fm(128, self.ktloc(4, 1)), ("tm", 256, 256, self.V_loc[2], False)]))
            blocks.append((4120, 512, [fm(u * 128, self.QT_d[16 + u]) for u in range(4)]))
            blocks.append((4632, 512, [fm(u * 128, self.QT_d[20 + u]) for u in range(4)]))
            blocks.append((5144, 512, [fm(0, self.ktloc(5, 0)), fm(128, self.ktloc(5, 1)), ("tm", 256, 256, self.V_loc[3], False)]))
            for i in range(12):
                blocks.append((5656 + i * 512, 512, [fm(u * 128, self.G_d[i * 4 + u], True) for u in range(4)]))

            kv_idx = [2, 3, 4, 8, 11]
            blocks = [blocks[i] for i in kv_idx] + [b for i, b in enumerate(blocks) if i not in kv_idx]
            n_kv = len(kv_idx)
            kvkeys = []

            def load(bi):
                col0, ncols, _ = blocks[bi]
                slot = slots[bi % NSL]
                src = self.w_in[layer, :, col0:col0 + ncols].rearrange("(k p) n -> p k n", p=128)
                for kh in range(2):
                    S.dma("pool", lambda q, slot=slot, src=src, kh=kh, ncols=ncols:
                          q.dma_start(out=slot[:, kh * 8:(kh + 1) * 8, 0:ncols], in_=src[:, kh * 8:(kh + 1) * 8, :]),
                          f"wslA{bi % NSL}", writes=[("wsl", bi % NSL)])

            PF = 2
            for bi in range(min(PF, len(blocks))):
                load(bi)
            ofm = 0
            ogt = 0
            otm = 0
            for bi, (col0, ncols, units) in enumerate(blocks):
                if bi + PF < len(blocks):
                    load(bi + PF)
                slot = slots[bi % NSL]
                skey = ("wsl", bi % NSL)
                for u in units:
                    if u[0] == "fm":
                        _, sub, dst, sig = u
                        if sig:
                            ob = GTo[ogt % 3]; okey = ("gto", ogt % 3); ogt += 1
                        else:
                            ob = FMo[ofm % 3]; okey = ("fmo", ofm % 3); ofm += 1
                        for half in range(2):
                            cols = slice(half * 512, (half + 1) * 512)
                            bank, bkey = self.ps()
                            for k in range(16):
                                S.op("pe", lambda q, bank=bank, slot=slot, k=k, sub=sub, cols=cols:
                                     q.matmul(bank[:], slot[:, k, sub:sub + 128], HT[:, k, cols], start=(k == 0), stop=(k == 15)),
                                     reads=[skey, ("HT", k)], writes=[bkey])
                            S.op("act", lambda q, bank=bank, ob=ob, cols=cols, sig=sig:
                                 q.activation(out=ob[:, cols], in_=bank[:], func=(AF.Sigmoid if sig else AF.Copy)),
                                 reads=[bkey], writes=[okey + (half,)])
                        S.dma("sp", lambda q, ob=ob, dst=dst: q.dma_start(out=dst, in_=ob[:]), "stA_" + "_".join(str(x) for x in okey),
                              reads=[okey + (0,), okey + (1,)], writes=([("KV_out", len(kvkeys))] if bi < n_kv else []))
                        if bi < n_kv:
                            kvkeys.append(("KV_out", len(kvkeys)))
                    else:
                        _, sub, n, dst, sig = u
                        if sig:
                            ob = AGo; okey = ("ago",)
                        else:
                            ob = TMo[otm % 2]; okey = ("tmo", otm % 2); otm += 1
                        for s in range(8):
                            bank, bkey = self.ps()
                            for k in range(16):
                                S.op("pe", lambda q, bank=bank, slot=slot, k=k, sub=sub, n=n, s=s:
                                     q.matmul(bank[:, 0:n], HT[:, k, s * 128:(s + 1) * 128], slot[:, k, sub:sub + n],
                                              start=(k == 0), stop=(k == 15)),
                                     reads=[skey, ("HT", k)], writes=[bkey])
                            S.op("act", lambda q, bank=bank, ob=ob, s=s, n=n, sig=sig:
                                 q.activation(out=ob[:, s, 0:n], in_=bank[:, 0:n], func=(AF.Sigmoid if sig else AF.Copy)),
                                 reads=[bkey], writes=[okey + (s,)])
                        dview = dst.rearrange("(s p) n -> p s n", p=128)
                        S.dma("sp", lambda q, ob=ob, dview=dview, n=n: q.dma_start(out=dview, in_=ob[:, :, 0:n]), "stA_" + "_".join(str(x) for x in okey),
                              reads=[okey + (s,) for s in range(8)], writes=([("KV_out", len(kvkeys))] if bi < n_kv else []))
                        if bi < n_kv:
                            kvkeys.append(("KV_out", len(kvkeys)))
                if bi == n_kv - 1 and self.mode == "fused":
                    self.exchange(kvkeys)
            S.barrier()
            S.flush()

    def compress(self, layer, kcT, vcc):
        S = self.S
        with ExitStack() as st:
            KC = [self.sb(st, f"KC{i}", [128, 2048], BF16) for i in range(2)]
            W1 = [self.sb(st, f"cw1{i}", [128, 32, 256], BF16) for i in range(2)]
            W2 = [self.sb(st, f"cw2{i}", [128, 2, 128], BF16) for i in range(2)]
            posT = self.sb(st, "posT", [128, 2, 32], F32)
            posB = [self.sb(st, f"posB{i}", [128, 32, 127], BF16) for i in range(2)]
            G = [self.sb(st, f"Gc{i}", [128, 2, 127], BF16) for i in range(2)]
            xs = [self.sb(st, f"gx{i}", [128, 127], F32) for i in range(2)]
            t1 = [self.sb(st, f"gt{i}", [128, 127], F32) for i in range(2)]
            for kv in range(2):
                S.dma("sp", lambda q, kv=kv: q.dma_start(out=posT[:, kv, :], in_=self.posT[layer, kv]), f"ld_pos{kv}", writes=[("posT", kv)])
                w1src = self.cmp_w1[layer, kv].rearrange("(l p) n -> p l n", p=128)
                for lh in range(2):
                    S.dma("pool", lambda q, kv=kv, lh=lh, w1src=w1src: q.dma_start(out=W1[kv][:, lh * 16:(lh + 1) * 16, :], in_=w1src[:, lh * 16:(lh + 1) * 16, :]),
                          f"cw1_{kv}", writes=[("cw1", kv)])
                S.dma("pool", lambda q, kv=kv: q.dma_start(out=W2[kv][:], in_=self.cmp_w2[layer, kv].rearrange("(c p) n -> p c n", p=128)),
                      f"cw2_{kv}", writes=[("cw2", kv)])
            it = 0
            for kv in range(2):
                for kvh in range(2):
                    i = it % 2
                    it += 1
                    kc = KC[i]
                    kcv = kc[:].rearrange("p (s j r) -> p s j r", s=8, j=2)
                    for j in range(2):
                        src = self.ktall(j, kv, kvh).rearrange("p (s r) -> p s r", r=128)
                        S.dma("sp", lambda q, kcv=kcv, j=j, src=src: q.dma_start(out=kcv[:, :, j, :], in_=src), f"ld_kc{i}", writes=[("KC", i)])
                    if kvh == 0:
                        S.op("dve", lambda q, kv=kv: q.tensor_copy(out=posB[kv][:], in_=posT[:, kv, :].unsqueeze(2).broadcast_to([128, 32, 127])),
                             reads=[("posT", kv)], writes=[("posB", kv)])
                    for hc in range(2):
                        bank, bkey = self.ps()
                        for l in range(32):
                            S.op("pe", lambda q, bank=bank, kv=kv, l=l, hc=hc, kc=kc: q.matmul(bank[:, 0:127], W1[kv][:, l, hc * 128:(hc + 1) * 128], kc[:, l:l + 2017:16],
                                                                                         start=(l == 0), stop=False),
                                 reads=[("cw1", kv), ("KC", i)], writes=[bkey])
                        for l in range(32):
                            S.op("pe", lambda q, bank=bank, kv=kv, l=l, hc=hc: q.matmul(bank[:, 0:127], W1[kv][:, l, hc * 128:(hc + 1) * 128], posB[kv][:, l, :],
                                                                                    start=False, stop=(l == 31)),
                                 reads=[("cw1", kv), ("posB", kv)], writes=[bkey])
                        x_, t_ = xs[hc], t1[hc]
                        xk, tk = ("gx", hc), ("gt", hc)
                        S.op("act", lambda q, bank=bank, x_=x_: q.activation(out=x_[:], in_=bank[:, 0:127], func=AF.Copy), reads=[bkey], writes=[xk])
                        S.op("dve", lambda q, x_=x_, t_=t_: q.tensor_tensor(out=t_[:], in0=x_[:], in1=x_[:], op=ALU.mult), reads=[xk], writes=[tk])
                        S.op("dve", lambda q, t_=t_: q.tensor_scalar(out=t_[:], in0=t_[:], scalar1=0.044715, scalar2=1.0, op0=ALU.mult, op1=ALU.add), reads=[tk], writes=[tk])
                        S.op("dve", lambda q, x_=x_, t_=t_: q.tensor_tensor(out=t_[:], in0=t_[:], in1=x_[:], op=ALU.mult), reads=[tk, xk], writes=[tk])
                        S.op("act", lambda q, t_=t_: q.activation(out=t_[:], in_=t_[:], func=AF.Sigmoid, scale=1.5957691216057308), reads=[tk], writes=[tk])
                        S.op("dve", lambda q, x_=x_, t_=t_, i=i, hc=hc: q.tensor_tensor(out=G[i][:, hc, :], in0=t_[:], in1=x_[:], op=ALU.mult),
                             reads=[tk, xk], writes=[("Gc", i, hc)])
                    bank, bkey = self.ps()
                    if kv == 0:
                        for hc in range(2):
                            S.op("pe", lambda q, bank=bank, hc=hc, i=i, kv=kv: q.matmul(bank[:, 0:127], W2[kv][:, hc, :], G[i][:, hc, :], start=(hc == 0), stop=(hc == 1)),
                                 reads=[("cw2", kv), ("Gc", i, hc)], writes=[bkey])
                        S.op("act", lambda q, bank=bank, kvh=kvh: q.activation(out=kcT[:, kvh, 0:127], in_=bank[:, 0:127], func=AF.Copy),
                             reads=[bkey], writes=[("kcT", kvh)])
                    else:
                        for hc in range(2):
                            S.op("pe", lambda q, bank=bank, hc=hc, i=i, kv=kv: q.matmul(bank[0:127, 0:128], G[i][:, hc, :], W2[kv][:, hc, :], start=(hc == 0), stop=(hc == 1)),
                                 reads=[("cw2", kv), ("Gc", i, hc)], writes=[bkey])
                        S.op("act", lambda q, bank=bank, kvh=kvh: q.activation(out=vcc[0:127, kvh, 0:128], in_=bank[0:127, 0:128], func=AF.Copy),
                             reads=[bkey], writes=[("vcc", kvh)])
            S.barrier()
            S.flush()

    def attn(self, R, qt, qkey, KT, kkey, V, vkey, kt0, kt1, near_start, nb, nbkey, nb_off, cfar_ap,
             mask=None, mkey=None, mblk=None, sink=None, cmp_slot=None, cmp_first=False, fin=None):
        S = self.S
        i = R["ac"] % NBUF
        k8 = R["ac"] % 8
        R["ac"] += 1
        L, Pb, PT, sm = R["L"][i], R["Pb"][i], R["PT"][i], R["sm"]
        Lk, Pk, Tk = ("L", i), ("Pb", i), ("PT", i)
        smk = ("sm", k8)
        n = (kt1 - kt0) * 128
        nfar = (near_start - kt0) * 128
        mbk = None
        if mask is not None and nfar > 0:
            mb = R["mb"]
            mbk = ("mb", k8)
            nfb = nfar // mblk
            S.op("dve", lambda q: q.tensor_scalar(out=mb[:, k8, 0:nfb], in0=mask[:, 0:nfb], scalar1=cfar_ap, scalar2=None, op0=ALU.add),
                 reads=[mkey, "cfar"], writes=[mbk])
        for c0 in range(0, n, 512):
            w = min(512, n - c0)
            bank, bkey = self.ps()
            S.op("pe", lambda q, bank=bank, w=w, c0=c0: q.matmul(bank[:, 0:w], qt, KT[:, kt0 * 128 + c0:kt0 * 128 + c0 + w], start=True, stop=True),
                 reads=[qkey, kkey], writes=[bkey])
            a, b = c0, min(c0 + w, nfar)
            if b > a:
                if mask is None:
                    S.op("dve", lambda q, bank=bank, a=a, b=b, c0=c0: q.tensor_scalar(out=L[:, a:b], in0=bank[:, a - c0:b - c0], scalar1=SCALE, scalar2=cfar_ap,
                                                                                     op0=ALU.mult, op1=ALU.add),
                         reads=[bkey, "cfar"], writes=[Lk])
                else:
                    nbk = (b - a) // mblk
                    S.op("dve", lambda q, bank=bank, a=a, b=b, c0=c0, nbk=nbk: q.scalar_tensor_tensor(
                        out=L[:, a:b].rearrange("p (b r) -> p b r", r=mblk), in0=bank[:, a - c0:b - c0].rearrange("p (b r) -> p b r", r=mblk), scalar=SCALE,
                        in1=R["mb"][:, k8, a // mblk:a // mblk + nbk].unsqueeze(2).broadcast_to([128, nbk, mblk]), op0=ALU.mult, op1=ALU.add),
                         reads=[bkey, mbk], writes=[Lk])
            a, b = max(c0, nfar), c0 + w
            if b > a:
                S.op("dve", lambda q, bank=bank, a=a, b=b, c0=c0: q.scalar_tensor_tensor(out=L[:, a:b], in0=bank[:, a - c0:b - c0], scalar=SCALE,
                                                                                        in1=nb[:, nb_off + a - nfar:nb_off + b - nfar],
                                                                                        op0=ALU.mult, op1=ALU.add),
                     reads=[bkey, nbkey], writes=[Lk])
        if mask is not None:
            nnb = (n - nfar) // mblk
            lv = L[:, nfar:n].rearrange("p (b r) -> p b r", r=mblk)
            S.op("dve", lambda q: q.tensor_tensor(out=lv, in0=lv, in1=mask[:, nfar // mblk:nfar // mblk + nnb].unsqueeze(2).broadcast_to([128, nnb, mblk]), op=ALU.add),
                 reads=[Lk, mkey], writes=[Lk])
        mk, nk = smk + ("m",), smk + ("negm",)
        S.op("dve", lambda q: q.reduce_max(out=sm[:, k8, 0:1], in_=L[:, 0:n], axis=AX.X), reads=[Lk], writes=[mk])
        clamp = sink if sink is not None else -30000.0
        S.op("dve", lambda q: q.tensor_scalar(out=sm[:, k8, 1:2], in0=sm[:, k8, 0:1], scalar1=clamp, scalar2=-1.0, op0=ALU.max, op1=ALU.mult),
             reads=[mk] + (["sinks"] if sink is not None else []), writes=[nk])
        negm = sm[:, k8, 1:2]

        def s2():
            if cmp_slot is not None:
                P32 = R["P32"]
                psum4 = R["psum4"]
                dk, rk = smk + ("d32",), smk + ("r32",)
                S.op("dve", lambda q: q.memset(sm[:, k8, 6:7], 0.0), writes=[dk])
                S.op("act", lambda q: q.activation(out=P32[:, 0:n], in_=L[:, 0:n], func=AF.Exp, bias=negm, scale=1.0, accum_out=sm[:, k8, 6:7]),
                     reads=[Lk, nk, dk], writes=["P32", dk])
                S.op("dve", lambda q: q.tensor_scalar(out=sm[:, k8, 7:8], in0=sm[:, k8, 6:7], scalar1=1e-30, scalar2=None, op0=ALU.max), reads=[dk], writes=[rk])
                S.op("dve", lambda q: q.reciprocal(out=sm[:, k8, 7:8], in_=sm[:, k8, 7:8]), reads=[rk], writes=[rk])
                p4k = ("psum4", cmp_slot)
                if cmp_first:
                    S.op("dve", lambda q: q.tensor_scalar(out=psum4[:, cmp_slot, :], in0=P32[:, 0:n], scalar1=sm[:, k8, 7:8], scalar2=None, op0=ALU.mult),
                         reads=["P32", rk], writes=[p4k])
                else:
                    S.op("dve", lambda q: q.scalar_tensor_tensor(out=psum4[:, cmp_slot, :], in0=P32[:, 0:n], scalar=sm[:, k8, 7:8], in1=psum4[:, cmp_slot, :],
                                                                 op0=ALU.mult, op1=ALU.add),
                         reads=["P32", rk, p4k], writes=[p4k])
                S.op("pool", lambda q: q.tensor_copy(out=Pb[:, 0:n], in_=P32[:, 0:n]), reads=["P32"], writes=[Pk])
            else:
                S.op("act", lambda q: q.activation(out=Pb[:, 0:n], in_=L[:, 0:n], func=AF.Exp, bias=negm, scale=1.0), reads=[Lk, nk], writes=[Pk])

        ntile = kt1 - kt0

        cl = cmp_slot is not None

        def s3():
            self._attn_s3(Pb, PT, Pk, Tk, ntile)

        def s4():
            self._attn_s4(PT, Tk, V, vkey, kt0, ntile, sink, negm, nk, sm, k8, smk, fin, cl)

        R["pend"].append([s2, s3, s4])
        self.attn_round(R)

    def attn_cmp4(self, R, QT, qkey, s, kvh, kcT, vcc, BC, bckey, AG, OA):
        S = self.S
        i = R["ac"] % NBUF
        k8 = R["ac"] % 8
        R["ac"] += 1
        L, Pb, PT, s4t, P32, psum4 = R["L"][i], R["Pb"][i], R["PT"][i], R["s4"], R["P32"], R["psum4"]
        Lk, Pk, Tk = ("L", i), ("Pb", i), ("PT", i)
        sk = ("s4", k8)
        m4, negm4, den4, r4 = s4t[:, k8, 0:4], s4t[:, k8, 4:8], s4t[:, k8, 8:12], s4t[:, k8, 12:16]
        bank, bkey = self.ps()
        for g in range(4):
            S.op("pe", lambda q, g=g: q.matmul(bank[:, g * 128:(g + 1) * 128], QT[:, g, s * 128:(s + 1) * 128], kcT[:, kvh, :], start=True, stop=True),
                 reads=[qkey, ("kcT", kvh)], writes=[bkey])
        S.op("dve", lambda q: q.scalar_tensor_tensor(out=L[:, 0:512], in0=bank[:], scalar=SCALE, in1=BC[:, kvh * 4:(kvh + 1) * 4, :].rearrange("p g c -> p (g c)"),
                                                     op0=ALU.mult, op1=ALU.add), reads=[bkey, bckey], writes=[Lk])
        S.op("dve", lambda q: q.reduce_max(out=m4, in_=L[:, 0:512].rearrange("p (g c) -> p g c", c=128), axis=AX.X), reads=[Lk], writes=[sk + ("m",)])
        S.op("dve", lambda q: q.tensor_scalar(out=negm4, in0=m4, scalar1=-30000.0, scalar2=-1.0, op0=ALU.max, op1=ALU.mult), reads=[sk + ("m",)], writes=[sk + ("negm",)])
        S.op("dve", lambda q: q.memset(den4, 0.0), writes=[sk + ("den",)])

        def s2():
            for g in range(4):
                S.op("act", lambda q, g=g: q.activation(out=P32[:, g * 128:(g + 1) * 128], in_=L[:, g * 128:(g + 1) * 128], func=AF.Exp, bias=s4t[:, k8, 4 + g:5 + g], scale=1.0,
                                                        accum_out=s4t[:, k8, 8 + g:9 + g]),
                     reads=[Lk, sk + ("negm",), sk + ("den",)], writes=["P32", sk + ("den",)])
            S.op("dve", lambda q: q.tensor_scalar(out=r4, in0=den4, scalar1=1e-30, scalar2=None, op0=ALU.max), reads=[sk + ("den",)], writes=[sk + ("r",)])
            S.op("dve", lambda q: q.reciprocal(out=r4, in_=r4), reads=[sk + ("r",)], writes=[sk + ("r",)])
            lv = L[:, 0:512].rearrange("p (g c) -> p g c", c=128)
            S.op("dve", lambda q: q.tensor_tensor(out=lv, in0=P32[:].rearrange("p (g c) -> p g c", c=128), in1=r4.unsqueeze(2).broadcast_to([128, 4, 128]), op=ALU.mult),
                 reads=["P32", sk + ("r",), Lk], writes=[Lk])
            S.op("dve", lambda q: q.reduce_sum(out=psum4[:, s, :], in_=L[:, 0:512].rearrange("p (g c) -> p c g", c=128), axis=AX.X), reads=[Lk], writes=[("psum4", s)])
            S.op("pool", lambda q: q.tensor_copy(out=Pb[:, 0:512], in_=P32[:]), reads=["P32"], writes=[Pk])

        def s3():
            bk, bkk = self.ps()
            bb = bk[:].bitcast(BF16)
            for g in range(4):
                S.op("pe", lambda q, g=g: q.transpose(bb[:, g * 128:(g + 1) * 128], Pb[:, g * 128:(g + 1) * 128], self.identb[:]), reads=[Pk, "identb"], writes=[bkk])
            S.op("act", lambda q: q.activation(out=PT[:, 0:512], in_=bb[:, 0:512], func=AF.Copy), reads=[bkk], writes=[Tk])

        def s4():
            bo, bok = self.ps()
            for g in range(4):
                S.op("pe", lambda q, g=g: q.matmul(bo[:, g * 128:(g + 1) * 128], PT[:, g * 128:(g + 1) * 128], vcc[:, kvh, 0:128], start=True, stop=True),
                     reads=[Tk, ("vcc", kvh)], writes=[bok])
            S.op("dve", lambda q: q.tensor_tensor(out=m4, in0=r4, in1=AG[:, s, kvh * 12:kvh * 12 + 12:3], op=ALU.mult), reads=[sk + ("r",), "AG"], writes=[sk + ("m",)])
            S.op("dve", lambda q: q.tensor_tensor(out=OA[:, s, :, :], in0=bo[:].rearrange("p (g c) -> p g c", c=128), in1=m4.unsqueeze(2).broadcast_to([128, 4, 128]), op=ALU.mult),
                 reads=[bok, sk + ("m",)], writes=[("OA", s, g) for g in range(4)])

        R["pend"].append([s2, s3, s4])
        self.attn_round(R)

    def attn_swa4(self, R, QT, qkey, s, KT, kkey, V, vkey, NB2, nbkeys, sink4, OA):
        S = self.S
        i = R["ac"] % NBUF
        k8 = R["ac"] % 8
        R["ac"] += 1
        L, Pb, PT, s4t = R["L"][i], R["Pb"][i], R["PT"][i], R["s4"]
        Lk, Pk, Tk = ("L", i), ("Pb", i), ("PT", i)
        sk = ("s4", k8)
        m4, negm4, den4, r4 = s4t[:, k8, 0:4], s4t[:, k8, 4:8], s4t[:, k8, 8:12], s4t[:, k8, 12:16]
        k0 = max(0, 2 * s - 1)
        ntile = 2 * s + 2 - k0
        n = ntile * 128
        off = (k0 - (2 * s - 1)) * 128
        for g in range(4):
            bank, bkey = self.ps()
            S.op("pe", lambda q, g=g, bank=bank: q.matmul(bank[:, 0:n], QT[:, g, s * 128:(s + 1) * 128], KT[:, k0 * 128:k0 * 128 + n], start=True, stop=True),
                 reads=[qkey, kkey], writes=[bkey])
            nbt = NB2[g // 2]
            S.op("dve", lambda q, g=g, bank=bank, nbt=nbt: q.scalar_tensor_tensor(out=L[:, g * 384:g * 384 + n], in0=bank[:, 0:n], scalar=SCALE,
                                                                                   in1=nbt[:, (g % 2) * 384 + off:(g % 2) * 384 + off + n], op0=ALU.mult, op1=ALU.add),
                 reads=[bkey, nbkeys[g // 2]], writes=[Lk])
        l3 = L[:, 0:1536].rearrange("p (g c) -> p g c", c=384)[:, :, 0:n]
        S.op("dve", lambda q: q.reduce_max(out=m4, in_=l3, axis=AX.X), reads=[Lk], writes=[sk + ("m",)])
        S.op("dve", lambda q: q.tensor_tensor(out=negm4, in0=m4, in1=sink4, op=ALU.max), reads=[sk + ("m",), "sinks"], writes=[sk + ("negm",)])
        S.op("dve", lambda q: q.tensor_scalar(out=negm4, in0=negm4, scalar1=-1.0, scalar2=None, op0=ALU.mult), reads=[sk + ("negm",)], writes=[sk + ("negm",)])
        S.op("dve", lambda q: q.memset(den4, 0.0), writes=[sk + ("den",)])

        def s2():
            for g in range(4):
                S.op("act", lambda q, g=g: q.activation(out=Pb[:, g * 384:g * 384 + n], in_=L[:, g * 384:g * 384 + n], func=AF.Exp, bias=s4t[:, k8, 4 + g:5 + g], scale=1.0,
                                                        accum_out=s4t[:, k8, 8 + g:9 + g]),
                     reads=[Lk, sk + ("negm",), sk + ("den",)], writes=[Pk, sk + ("den",)])
            S.op("dve", lambda q: q.tensor_tensor(out=m4, in0=negm4, in1=sink4, op=ALU.add), reads=[sk + ("negm",), "sinks"], writes=[sk + ("m",)])
            S.op("act", lambda q: q.activation(out=m4, in_=m4, func=AF.Exp), reads=[sk + ("m",)], writes=[sk + ("m",)])
            S.op("dve", lambda q: q.tensor_tensor(out=r4, in0=den4, in1=m4, op=ALU.add), reads=[sk + ("den",), sk + ("m",)], writes=[sk + ("r",)])
            S.op("dve", lambda q: q.reciprocal(out=r4, in_=r4), reads=[sk + ("r",)], writes=[sk + ("r",)])

        def s3():
            tot = 4 * ntile
            for t0 in range(0, tot, 8):
                tn = min(8, tot - t0)
                bk, bkk = self.ps()
                bb = bk[:].bitcast(BF16)
                for u in range(t0, t0 + tn):
                    g, t = divmod(u, ntile)
                    S.op("pe", lambda q, bb=bb, u=u, t0=t0, g=g, t=t: q.transpose(bb[:, (u - t0) * 128:(u - t0 + 1) * 128], Pb[:, g * 384 + t * 128:g * 384 + (t + 1) * 128], self.identb[:]),
                         reads=[Pk, "identb"], writes=[bkk])
                S.op("act", lambda q, bb=bb, t0=t0, tn=tn: q.activation(out=PT[:, t0 * 128:(t0 + tn) * 128], in_=bb[:, 0:tn * 128], func=AF.Copy), reads=[bkk], writes=[Tk])

        def s4():
            bo, bok = self.ps()
            for g in range(4):
                for t in range(ntile):
                    u = g * ntile + t
                    S.op("pe", lambda q, g=g, t=t, u=u: q.matmul(bo[:, g * 128:(g + 1) * 128], PT[:, u * 128:(u + 1) * 128], V[:, k0 + t, 0:128], start=(t == 0), stop=(t == ntile - 1)),
                         reads=[Tk, vkey], writes=[bok])
            S.op("dve", lambda q: q.tensor_tensor(out=OA[:, s, :, :], in0=bo[:].rearrange("p (g c) -> p g c", c=128), in1=r4.unsqueeze(2).broadcast_to([128, 4, 128]), op=ALU.mult),
                 reads=[bok, sk + ("r",)], writes=[("OA", s, g) for g in range(4)])

        R["pend"].append([s2, s3, s4])
        self.attn_round(R)

    def attn_round(self, R):
        pend = R["pend"]
        n = len(pend)
        for idx, stage in ((n - 2, 0), (n - 3, 1), (n - 4, 2)):
            if idx >= 0 and pend[idx] is not None:
                pend[idx][stage]()
        if n >= 4:
            pend.pop(0)

    def attn_bubble(self, R):
        R["pend"].append(None)
        self.attn_round(R)

    def attn_marker(self, R, fn):
        noop = lambda: None
        R["pend"].append([noop, noop, fn])
        self.attn_round(R)

    def attn_drain(self, R):
        pend = R["pend"]
        while any(p is not None for p in pend):
            self.attn_bubble(R)
        pend.clear()

    def _attn_s3(self, Pb, PT, Pk, Tk, ntile):
        S = self.S
        for t0 in range(0, ntile, 8):
            tn = min(8, ntile - t0)
            bank, bkey = self.ps()
            bb = bank[:].bitcast(BF16)
            for t in range(t0, t0 + tn):
                S.op("pe", lambda q, bb=bb, t=t, t0=t0: q.transpose(bb[:, (t - t0) * 128:(t - t0 + 1) * 128], Pb[:, t * 128:(t + 1) * 128], self.identb[:]),
                     reads=[Pk, "identb"], writes=[bkey])
            S.op("act", lambda q, bb=bb, t0=t0, tn=tn: q.activation(out=PT[:, t0 * 128:(t0 + tn) * 128], in_=bb[:, 0:tn * 128], func=AF.Copy),
                 reads=[bkey], writes=[Tk])

    def _attn_s4(self, PT, Tk, V, vkey, kt0, ntile, sink, negm, nk, sm, k8, smk, fin, clampden):
        S = self.S
        bank_o, okey = self.ps()
        for t in range(ntile):
            S.op("pe", lambda q, t=t: q.matmul(bank_o[:, 0:129], PT[:, t * 128:(t + 1) * 128], V[:, kt0 + t, 0:129], start=(t == 0), stop=(t == ntile - 1)),
                 reads=[Tk, vkey], writes=[okey])
        dk2, rk2 = smk + ("den",), smk + ("r",)
        if sink is not None:
            ek = smk + ("es",)
            S.op("act", lambda q: q.activation(out=sm[:, k8, 5:6], in_=sink, func=AF.Exp, bias=negm, scale=1.0), reads=[nk, "sinks"], writes=[ek])
            S.op("dve", lambda q: q.tensor_tensor(out=sm[:, k8, 2:3], in0=bank_o[:, 128:129], in1=sm[:, k8, 5:6], op=ALU.add), reads=[okey, ek], writes=[dk2])
            S.op("dve", lambda q: q.reciprocal(out=sm[:, k8, 3:4], in_=sm[:, k8, 2:3]), reads=[dk2], writes=[rk2])
        elif clampden:
            S.op("dve", lambda q: q.tensor_scalar(out=sm[:, k8, 2:3], in0=bank_o[:, 128:129], scalar1=1e-30, scalar2=None, op0=ALU.max), reads=[okey], writes=[dk2])
            S.op("dve", lambda q: q.reciprocal(out=sm[:, k8, 3:4], in_=sm[:, k8, 2:3]), reads=[dk2], writes=[rk2])
        else:
            S.op("dve", lambda q: q.reciprocal(out=sm[:, k8, 3:4], in_=bank_o[:, 128:129]), reads=[okey], writes=[rk2])
        fin(bank_o, okey, sm, k8, smk)

    def phase_B(self, layer):
        S = self.S
        with ExitStack() as st:
            cfar = self.sb(st, "cfar", [128, 24], F32)
            share = self.sb(st, "share", [128, 32], F32)
            selkeep = self.sb(st, "selkeep", [128, 8, 32], F32)
            seladd = self.sb(st, "seladd", [128, 8, 32], F32)
            mvalid = self.sb(st, "mvalid", [128, 8, 8], F32)
            madd = self.sb(st, "madd", [128, 8, 8], F32)
            mown = self.sb(st, "mown", [128, 8, 8], F32)
            AG = self.sb(st, "AG", [128, 8, 24], F32)
            sinks = self.sb(st, "sinks", [128, self.LW * 8], F32)
            for name, t_ in (("cfar", cfar), ("share", share), ("selkeep", selkeep), ("seladd", seladd), ("mvalid", mvalid), ("madd", madd), ("mown", mown)):
                S.dma("sp", lambda q, t_=t_, name=name: q.dma_start(out=t_[:], in_=self.tb[name]), "ld_t_" + name, writes=[name])
            S.dma("sp", lambda q: q.dma_start(out=sinks[:], in_=self.sinks_in[:, :]), "ld_sinks", writes=["sinks"])
            S.dma("sp", lambda q: q.dma_start(out=AG[:], in_=self.AG_d.rearrange("(s p) n -> p s n", p=128)), "ld_AG", writes=["AG"])
            kcT = self.sb(st, "kcT", [128, 2, 128], BF16)
            vcc = self.sb(st, "vcc", [128, 2, 130], BF16)
            S.op("dve", lambda q: q.memset(kcT[:], 0.0), writes=[("kcT", 0), ("kcT", 1)])
            S.op("dve", lambda q: q.memset(vcc[:], 0.0), writes=[("vcc", 0), ("vcc", 1), "vcc1"])
            S.op("dve", lambda q: q.memset(vcc[:, :, 128:129], 1.0), writes=["vcc1"])
            self.compress(layer, kcT, vcc)
            KTr = [self.sb(st, f"KTr{i}", [128, 2048], BF16) for i in range(3)]
            Vr = [self.sb(st, f"Vr{i}", [128, 16, 130], BF16) for i in range(3)]
            for i in range(3):
                S.op("pool", lambda q, i=i: q.memset(Vr[i][:, :, 128:130], 1.0), writes=[("Vone", i)])
            QTg = [self.sb(st, f"QTg{i}", [128, 4, NT], BF16) for i in range(2)]
            NBr = [self.sb(st, f"NBr{i}", [128, 768], F32) for i in range(2)]
            BCr = [self.sb(st, f"BCr{i}", [128, 8, 128], F32) for i in range(2)]
            R = {"ac": 0,
                 "pend": [],
                 "L": [self.sb(st, f"L{i}", [128, 2048], F32) for i in range(NBUF)],
                 "Pb": [self.sb(st, f"Pb{i}", [128, 2048], BF16) for i in range(NBUF)],
                 "PT": [self.sb(st, f"PT{i}", [128, 2048], BF16) for i in range(NBUF)],
                 "sm": self.sb(st, "sm", [128, 8, 8], F32),
                 "mb": self.sb(st, "mb", [128, 8, 32], F32),
                 "P32": self.sb(st, "P32", [128, 512], F32),
                 "s4": self.sb(st, "s4", [128, 8, 16], F32),
                 "psum4": self.sb(st, "psum4", [128, 8, 128], F32)}
            OA = self.sb(st, "OA", [128, 8, 4, 128], F32)
            selm = self.sb(st, "selm", [128, 8, 32], F32)
            sc = self.sb(st, "sc", [128, 8, 32], F32)
            cm = [self.sb(st, "cm0", [128, 32, 32], F32)]
            cnt = self.sb(st, "cnt", [128, 8, 32], F32)
            p4Tall = self.sb(st, "p4T", [128, 8, 128], F32)
            kmf = self.sb(st, "kmf", [128, 8], F32)
            kmT = self.sb(st, "kmT", [128, 8], BF16)
            mm = [self.sb(st, f"mm{i}", [128, 8, 16], F32) for i in range(2)]
            msc = self.sb(st, "msc", [128, 64], F32)
            mcm = self.sb(st, "mcm", [128, 8, 8, 8], F32)
            mcnt = self.sb(st, "mcnt", [128, 8, 8], F32)
            OTs = [self.sb(st, f"OTs{i}", [128, 4, 128], BF16) for i in range(4)]
            rgb = self.sb(st, "rgb", [128, 8], F32)
            cnts = {"k": 0, "nb": 0, "bc": 0, "grp": 0, "ot": 0, "mm": 0}

            def load_kv(ktyp, vtyp, kvh):
                i = cnts["k"] % 3
                cnts["k"] += 1
                kt, v = KTr[i], Vr[i]
                ktv = kt[:].rearrange("p (s j r) -> p s j r", s=8, j=2)
                vv = v[:, :, 0:128].rearrange("p (s j) r -> p s j r", j=2)
                for j in range(2):
                    src = self.ktall(j, ktyp, kvh).rearrange("p (s r) -> p s r", r=128)
                    S.dma("sp", lambda q, ktv=ktv, j=j, src=src: q.dma_start(out=ktv[:, :, j, :], in_=src), f"ld_kt{i}", writes=[("KT", i)])
                    vsrc = self.V_all[j, vtyp].rearrange("(s p) n -> p s n", p=128)[:, :, kvh * 128:(kvh + 1) * 128]
                    S.dma("sp", lambda q, vv=vv, j=j, vsrc=vsrc: q.dma_start(out=vv[:, :, j, :], in_=vsrc), f"ld_v{i}", writes=[("V", i)])
                return kt, ("KT", i), v, ("V", i), ("Vone", i)

            def load_nb(name, h, width):
                i = cnts["nb"] % 2
                cnts["nb"] += 1
                S.dma("sp", lambda q, i=i: q.dma_start(out=NBr[i][:, 0:width], in_=self.tb[name][:, h, :]), f"ld_nb{i}", writes=[("NB", i)])
                return NBr[i], ("NB", i)

            for m in range(3):
                for kvh in range(2):
                    gi = cnts["grp"] % 2
                    cnts["grp"] += 1
                    QT = QTg[gi]
                    h0 = m * 8 + kvh * 4
                    S.dma("sp", lambda q, QT=QT, h0=h0: q.dma_start(out=QT[:], in_=self.QT_d[h0:h0 + 4].rearrange("h p l -> p h l")), f"ld_q{gi}",
                          writes=[("QT", gi)])
                    qkey = ("QT", gi)

                    def qt_ap(g, s, QT=QT):
                        return QT[:, g, s * 128:(s + 1) * 128]

                    def fin_first(s, g, gate):
                        def fin(bank_o, okey, sm, k8, smk):
                            oak = ("OA", s, g)
                            if gate is None:
                                S.op("dve", lambda q: q.tensor_scalar(out=OA[:, s, g, :], in0=bank_o[:, 0:128], scalar1=sm[:, k8, 3:4], scalar2=None, op0=ALU.mult),
                                     reads=[okey, smk + ("r",)], writes=[oak])
                            else:
                                S.op("dve", lambda q: q.tensor_tensor(out=sm[:, k8, 4:5], in0=sm[:, k8, 3:4], in1=gate, op=ALU.mult),
                                     reads=[smk + ("r",), "AG"], writes=[smk + ("rg",)])
                                S.op("dve", lambda q: q.tensor_scalar(out=OA[:, s, g, :], in0=bank_o[:, 0:128], scalar1=sm[:, k8, 4:5], scalar2=None, op0=ALU.mult),
                                     reads=[okey, smk + ("rg",)], writes=[oak])
                        return fin

                    def fin_acc(s, g, gate):
                        def fin(bank_o, okey, sm, k8, smk):
                            oak = ("OA", s, g)
                            S.op("dve", lambda q: q.tensor_tensor(out=sm[:, k8, 4:5], in0=sm[:, k8, 3:4], in1=gate, op=ALU.mult),
                                 reads=[smk + ("r",), "AG"], writes=[smk + ("rg",)])
                            S.op("dve", lambda q: q.scalar_tensor_tensor(out=OA[:, s, g, :], in0=bank_o[:, 0:128], scalar=sm[:, k8, 4:5], in1=OA[:, s, g, :],
                                                                         op0=ALU.mult, op1=ALU.add),
                                 reads=[okey, smk + ("rg",), oak], writes=[oak])
                        return fin

                    if m == 0:
                        for s in range(8):
                            bi = cnts["bc"] % 2
                            cnts["bc"] += 1
                            S.dma("sp", lambda q, bi=bi, s=s: q.dma_start(out=BCr[bi][:], in_=self.tb["bias_c"][s]), f"ld_bc{bi}", writes=[("BC", bi)])
                            self.attn_cmp4(R, QT, qkey, s, kvh, kcT, vcc, BCr[bi], ("BC", bi), AG, OA)
                        self.attn_bubble(R)
                        p4T = p4Tall
                        for half in range(2):
                            bank, bkey = self.ps()
                            for s4 in range(4):
                                s = half * 4 + s4
                                S.op("pe", lambda q, bank=bank, s=s, s4=s4: q.transpose(bank[:, s4 * 128:(s4 + 1) * 128], R["psum4"][:, s, :], self.identf[:]),
                                     reads=[("psum4", s), "identf"], writes=[bkey])
                            S.op("act", lambda q, bank=bank, half=half: q.activation(out=p4T[:, half * 4:(half + 1) * 4, :].rearrange("p s c -> p (s c)"), in_=bank[:], func=AF.Copy),
                                 reads=[bkey], writes=[("p4T", half)])
                        bank2, bkey2 = self.ps()
                        for s in range(8):
                            S.op("pe", lambda q, bank2=bank2, s=s: q.matmul(bank2[:, s * 32:(s + 1) * 32], p4T[:, s, :], share[:], start=True, stop=True),
                                 reads=[("p4T", s // 4), "share"], writes=[bkey2])
                        sc2 = sc[:].rearrange("p s b -> p (s b)")
                        S.op("dve", lambda q, bank2=bank2: q.tensor_tensor(out=sc2, in0=bank2[:, 0:256], in1=selkeep[:].rearrange("p s b -> p (s b)"), op=ALU.mult),
                             reads=[bkey2, "selkeep"], writes=["sc"])
                        S.op("dve", lambda q: q.tensor_tensor(out=sc2, in0=sc2, in1=seladd[:].rearrange("p s b -> p (s b)"), op=ALU.add), reads=["sc", "seladd"], writes=["sc"])
                        for s in range(8):
                            ci = 0
                            S.op("dve", lambda q, s=s, ci=ci: q.tensor_tensor(out=cm[ci][:], in0=sc[:, s, :].unsqueeze(1).broadcast_to([128, 32, 32]),
                                                                              in1=sc[:, s, :].unsqueeze(2).broadcast_to([128, 32, 32]), op=ALU.is_gt),
                                 reads=["sc"], writes=[("cm", ci)])
                            S.op("dve", lambda q, s=s, ci=ci: q.reduce_sum(out=cnt[:, s, :], in_=cm[ci][:], axis=AX.X), reads=[("cm", ci)], writes=[("cnt", s)])
                        S.op("dve", lambda q: q.tensor_scalar(out=selm[:].rearrange("p s b -> p (s b)"), in0=cnt[:].rearrange("p s b -> p (s b)"), scalar1=15.5, scalar2=NEG,
                                                              op0=ALU.is_gt, op1=ALU.mult),
                             reads=[("cnt", s) for s in range(8)], writes=[("selm", s) for s in range(8)])
                        kt, kkey, v, vkey, vone = load_kv(2, 0, kvh)
                        for g in range(4):
                            h = kvh * 4 + g
                            nb, nbkey = load_nb("nb_sel", h, 384)
                            for s in range(8):
                                ns = max(0, 2 * s - 1)
                                self.attn(R, qt_ap(g, s), qkey, kt, kkey, v, vkey, 0, 2 * s + 2, ns, nb, nbkey, (ns - (2 * s - 1)) * 128, cfar[:, h:h + 1],
                                          mask=selm[:, s, :], mkey=("selm", s), mblk=64, fin=fin_acc(s, g, AG[:, s, h * 3 + 1:h * 3 + 2]))
                        kt, kkey, v, vkey, vone = load_kv(3, 1, kvh)
                        for g in range(4):
                            h = kvh * 4 + g
                            nb, nbkey = load_nb("nb_win", h, 768)
                            for s in range(8):
                                k0 = max(0, 2 * s - 4)
                                self.attn(R, qt_ap(g, s), qkey, kt, kkey, v, vkey, k0, 2 * s + 2, k0, nb, nbkey, (k0 - (2 * s - 4)) * 128, None,
                                          fin=fin_acc(s, g, AG[:, s, h * 3 + 2:h * 3 + 3]))
                    elif m == 1:
                        kt, kkey, v, vkey, vone = load_kv(4, 2, kvh)
                        for hh in range(2):
                            S.dma("sp", lambda q, hh=hh, kvh=kvh: q.dma_start(out=NBr[hh][:, 0:768].rearrange("p (g c) -> p g c", c=384),
                                                                    in_=self.tb["nb_swa"][:, kvh * 4 + hh * 2:kvh * 4 + hh * 2 + 2, :]),
                                  f"ld_nb{hh}", writes=[("NB", hh)])
                        cnts["nb"] = 0
                        sink4 = sinks[:, layer * 8 + kvh * 4:layer * 8 + kvh * 4 + 4]
                        for s in range(8):
                            self.attn_swa4(R, QT, qkey, s, kt, kkey, v, vkey, NBr, [("NB", 0), ("NB", 1)], sink4, OA)
                    else:
                        kt, kkey, v, vkey, vone = load_kv(5, 3, kvh)
                        S.op("dve", lambda q, kt=kt: q.reduce_sum(out=kmf[:], in_=kt[:].rearrange("p (b r) -> p b r", r=256), axis=AX.X), reads=[kkey], writes=["kmf"])
                        S.op("dve", lambda q: q.tensor_scalar(out=kmT[:], in0=kmf[:], scalar1=1.0 / 256.0, scalar2=None, op0=ALU.mult), reads=["kmf"], writes=["kmT"])
                        for g in range(4):
                            h = kvh * 4 + g
                            nb, nbkey = load_nb("nb_moba", h, 384)
                            hi = cnts["mm"] % 2
                            cnts["mm"] += 1
                            m16k = ("m16", hi)
                            bank, bkey = self.ps()
                            for s in range(8):
                                qa = qt_ap(g, s)
                                S.op("pe", lambda q, bank=bank, qa=qa, s=s: q.matmul(bank[:, s * 8:(s + 1) * 8], qa, kmT[:], start=True, stop=True),
                                     reads=[qkey, "kmT"], writes=[bkey])
                            mv64 = mvalid[:].rearrange("p s n -> p (s n)")
                            S.op("dve", lambda q, bank=bank, mv64=mv64: q.tensor_tensor(out=msc[:], in0=bank[:, 0:64], in1=mv64, op=ALU.mult), reads=[bkey, "mvalid"], writes=["msc"])
                            S.op("dve", lambda q: q.tensor_tensor(out=msc[:], in0=msc[:], in1=madd[:].rearrange("p s n -> p (s n)"), op=ALU.add), reads=["msc", "madd"], writes=["msc"])
                            ms3 = msc[:].rearrange("p (s n) -> p s n", n=8)
                            S.op("dve", lambda q, ms3=ms3: q.tensor_tensor(out=mcm[:], in0=ms3.unsqueeze(2).broadcast_to([128, 8, 8, 8]),
                                                                           in1=ms3.unsqueeze(3).broadcast_to([128, 8, 8, 8]), op=ALU.is_gt), reads=["msc"], writes=["mcm"])
                            S.op("dve", lambda q: q.reduce_sum(out=mcnt[:], in_=mcm[:], axis=AX.X), reads=["mcm"], writes=["mcnt"])
                            S.op("dve", lambda q: q.tensor_scalar(out=mcnt[:], in0=mcnt[:], scalar1=2.5, scalar2=None, op0=ALU.is_lt), reads=["mcnt"], writes=["mcnt"])
                            S.op("dve", lambda q: q.tensor_tensor(out=mcnt[:], in0=mcnt[:], in1=mvalid[:], op=ALU.mult), reads=["mcnt", "mvalid"], writes=["mcnt"])
                            S.op("dve", lambda q: q.tensor_tensor(out=mcnt[:], in0=mcnt[:], in1=mown[:], op=ALU.add), reads=["mcnt", "mown"], writes=["mcnt"])
                            S.op("dve", lambda q: q.tensor_scalar(out=mcnt[:], in0=mcnt[:], scalar1=1e30, scalar2=NEG, op0=ALU.mult, op1=ALU.add), reads=["mcnt"], writes=["mcnt"])
                            m16 = mm[hi]
                            S.op("dve", lambda q, m16=m16: q.tensor_copy(out=m16[:].rearrange("p s (n two) -> p (s n) two", two=2),
                                                                        in_=mcnt[:].rearrange("p s n -> p (s n)").unsqueeze(2).broadcast_to([128, 64, 2])),
                                 reads=["mcnt"], writes=[m16k])
                            if self.mode == "B":
                                S.dma("sp", lambda q, m16=m16, h=h: q.dma_start(out=self.dbg[h], in_=m16[:]), "st_dbg", reads=[m16k])
                            for s in range(8):
                                ns = max(0, 2 * s - 1)
                                self.attn(R, qt_ap(g, s), qkey, kt, kkey, v, vkey, 0, 2 * s + 2, ns, nb, nbkey, (ns - (2 * s - 1)) * 128,
                                          cfar[:, 16 + h:16 + h + 1], mask=m16[:, s, :], mkey=m16k, mblk=128, fin=fin_first(s, g, None))
                    def output_phase(h0=h0):
                        for s in range(8):
                            oi = cnts["ot"] % 4
                            cnts["ot"] += 1
                            bank, bkey = self.ps()
                            for g in range(4):
                                S.op("pe", lambda q, bank=bank, g=g, s=s: q.transpose(bank[:, g * 128:(g + 1) * 128], OA[:, s, g, :], self.identf[:]),
                                     reads=[("OA", s, g), "identf"], writes=[bkey])
                            S.op("act", lambda q, bank=bank, oi=oi: q.activation(out=OTs[oi][:].rearrange("p g r -> p (g r)"), in_=bank[:], func=AF.Copy),
                                 reads=[bkey], writes=[("OTs", oi)])
                            dst = self.OT_d[h0:h0 + 4, :, s * 128:(s + 1) * 128].rearrange("h p l -> p h l")
                            S.dma("sp", lambda q, oi=oi, dst=dst: q.dma_start(out=dst, in_=OTs[oi][:]), f"st_ot{oi}", reads=[("OTs", oi)])

                    self.attn_marker(R, output_phase)
            self.attn_drain(R)
            S.barrier()
            S.flush()

    def phase_C(self, layer):
        S = self.S
        with ExitStack() as st:
            OTh = self.sb(st, "OTh", [128, 24, 512], BF16)
            ZT = self.sb(st, "ZT", [128, 16, 512], BF16)
            Gs = [self.sb(st, f"Gs{i}", [128, 512], F32) for i in range(6)]
            WB = [self.sb(st, f"WB{i}", [128, 8, 512], BF16) for i in range(6)]
            WO = [self.sb(st, f"WO{i}", [128, 16, 512], BF16) for i in range(2)]
            zacc = [self.sb(st, f"zacc{i}", [128, 512], F32) for i in range(2)]
            tmp = [self.sb(st, f"ztmp{i}", [128, 512], F32) for i in range(2)]
            cn = {"g": 0, "wb": 0, "wo": 0, "z": 0, "t": 0}
            steps = []
            for half in range(2):
                cols = slice(half * 512, (half + 1) * 512)

                def ld_ot(half=half, cols=cols):
                    for m in range(3):
                        S.dma("sp", lambda q, m=m: q.dma_start(out=OTh[:, m * 8:(m + 1) * 8, :], in_=self.OT_d[m * 8:(m + 1) * 8, :, cols].rearrange("h p l -> p h l")),
                              f"ld_oth{m}", writes=[("OTh", m)])

                for fg in range(4):
                    box = {}

                    def ld(fg=fg, box=box, half=half, cols=cols, ld_ot=ld_ot):
                        if fg == 0:
                            ld_ot()
                        box["wb"] = []
                        for m in range(3):
                            wi = cn["wb"] % 6
                            cn["wb"] += 1
                            src = self.w_branch[layer, m, :, fg * 512:(fg + 1) * 512].rearrange("(h p) n -> p h n", p=128)
                            S.dma("pool", lambda q, wi=wi, src=src: q.dma_start(out=WB[wi][:], in_=src), f"ld_wb{wi}", writes=[("WB", wi)])
                            box["wb"].append(wi)

                    def comp(fg=fg, box=box, half=half, cols=cols):
                        for fi in range(4):
                            f = fg * 4 + fi
                            zi = cn["z"] % 2
                            cn["z"] += 1
                            for m in range(3):
                                gi = cn["g"] % 6
                                cn["g"] += 1
                                S.dma("sp", lambda q, gi=gi, m=m, f=f: q.dma_start(out=Gs[gi][:], in_=self.G_d[m * 16 + f, :, cols]), f"ld_g{gi}", writes=[("Gs", gi)])
                                wi = box["wb"][m]
                                bank, bkey = self.ps()
                                for h in range(8):
                                    S.op("pe", lambda q, bank=bank, wi=wi, h=h, fi=fi, m=m: q.matmul(bank[:], WB[wi][:, h, fi * 128:(fi + 1) * 128], OTh[:, m * 8 + h, :],
                                                                                                 start=(h == 0), stop=(h == 7)),
                                         reads=[("WB", wi), ("OTh", m)], writes=[bkey])
                                if m == 0:
                                    S.op("dve", lambda q, bank=bank, zi=zi, gi=gi: q.tensor_tensor(out=zacc[zi][:], in0=bank[:], in1=Gs[gi][:], op=ALU.mult),
                                         reads=[bkey, ("Gs", gi)], writes=[("zacc", zi)])
                                else:
                                    ti = cn["t"] % 2
                                    cn["t"] += 1
                                    S.op("dve", lambda q, bank=bank, ti=ti, gi=gi: q.tensor_tensor(out=tmp[ti][:], in0=bank[:], in1=Gs[gi][:], op=ALU.mult),
                                         reads=[bkey, ("Gs", gi)], writes=[("ztmp", ti)])
                                    if m == 1:
                                        S.op("pool", lambda q, zi=zi, ti=ti: q.tensor_tensor(out=zacc[zi][:], in0=zacc[zi][:], in1=tmp[ti][:], op=ALU.add),
                                             reads=[("zacc", zi), ("ztmp", ti)], writes=[("zacc", zi)])
                                    else:
                                        S.op("pool", lambda q, zi=zi, ti=ti, f=f: q.tensor_tensor(out=ZT[:, f, :], in0=zacc[zi][:], in1=tmp[ti][:], op=ALU.add),
                                             reads=[("zacc", zi), ("ztmp", ti)], writes=[("ZT", f)])

                    steps.append((ld, comp))
                for fo in range(4):
                    box = {}

                    def ld(fo=fo, box=box):
                        wi = cn["wo"] % 2
                        cn["wo"] += 1
                        src = self.w_out[layer, :, fo * 512:(fo + 1) * 512].rearrange("(k p) n -> p k n", p=128)
                        for kh in range(2):
                            S.dma("pool", lambda q, wi=wi, src=src, kh=kh: q.dma_start(out=WO[wi][:, kh * 8:(kh + 1) * 8, :], in_=src[:, kh * 8:(kh + 1) * 8, :]),
                                  f"ld_wo{wi}", writes=[("WO", wi)])
                        box["wi"] = wi

                    def comp(fo=fo, box=box, cols=cols):
                        wi = box["wi"]
                        for fi in range(4):
                            f2 = fo * 4 + fi
                            bank, bkey = self.ps()
                            for k in range(16):
                                S.op("pe", lambda q, bank=bank, wi=wi, k=k, fi=fi: q.matmul(bank[:], WO[wi][:, k, fi * 128:(fi + 1) * 128], ZT[:, k, :], start=(k == 0), stop=(k == 15)),
                                     reads=[("WO", wi), ("ZT", k)], writes=[bkey])
                            S.op("dve", lambda q, bank=bank, f2=f2: q.tensor_tensor(out=self.XT[:, f2, cols], in0=bank[:], in1=self.XT[:, f2, cols], op=ALU.add),
                                 reads=[bkey, ("XT", f2)], writes=[("XT", f2)])

                    steps.append((ld, comp))
            steps[0][0]()
            for i, (ld, comp) in enumerate(steps):
                if i + 1 < len(steps):
                    steps[i + 1][0]()
                comp()
            S.barrier()
            S.flush()

    def phase_D(self, layer):
        S = self.S
        with ExitStack() as st:
            H2 = self.sb(st, "H2", [128, 16, NT], BF16)
            self.rmsnorm(st, layer * 2 + 1, H2, "H2")
            W1r = [self.sb(st, f"W1r{i}", [128, 16, 512], BF16) for i in range(2)]
            W2r = [self.sb(st, f"W2r{i}", [128, 4, D], BF16) for i in range(2)]
            U = [self.sb(st, f"U{i}", [128, 4, NT], BF16) for i in range(2)]
            sq = [self.sb(st, f"fsq{i}", [128, 512], F32) for i in range(2)]
            cn = {"sq": 0}

            def ld(g):
                wi = g % 2
                src1 = self.w_mlp_in[layer, :, g * 512:(g + 1) * 512].rearrange("(k p) n -> p k n", p=128)
                for kh in range(2):
                    S.dma("pool", lambda q, wi=wi, src1=src1, kh=kh: q.dma_start(out=W1r[wi][:, kh * 8:(kh + 1) * 8, :], in_=src1[:, kh * 8:(kh + 1) * 8, :]),
                          f"ld_w1{wi}", writes=[("W1r", wi)])
                src2 = self.w_mlp_out[layer, g * 512:(g + 1) * 512, :].rearrange("(c p) n -> p c n", p=128)
                for ch in range(2):
                    S.dma("pool", lambda q, wi=wi, src2=src2, ch=ch: q.dma_start(out=W2r[wi][:, ch * 2:(ch + 1) * 2, :], in_=src2[:, ch * 2:(ch + 1) * 2, :]),
                          f"ld_w2{wi}", writes=[("W2r", wi)])

            ld(0)
            for g in range(16):
                if g + 1 < 16:
                    ld(g + 1)
                wi = g % 2
                ui = g % 2
                for hc in range(4):
                    for half in range(2):
                        cols = slice(half * 512, (half + 1) * 512)
                        bank, bkey = self.ps()
                        for k in range(16):
                            S.op("pe", lambda q, bank=bank, wi=wi, k=k, hc=hc, cols=cols: q.matmul(bank[:], W1r[wi][:, k, hc * 128:(hc + 1) * 128], H2[:, k, cols],
                                                                                            start=(k == 0), stop=(k == 15)),
                                 reads=[("W1r", wi), ("H2", k)], writes=[bkey])
                        si = cn["sq"] % 2
                        cn["sq"] += 1
                        S.op("act", lambda q, bank=bank, si=si: q.activation(out=sq[si][:], in_=bank[:], func=AF.Square), reads=[bkey], writes=[("fsq", si)])
                        S.op("dve", lambda q, bank=bank, si=si, ui=ui, hc=hc, cols=cols: q.scalar_tensor_tensor(out=U[ui][:, hc, cols], in0=bank[:], scalar=0.0, in1=sq[si][:],
                                                                                                           op0=ALU.is_gt, op1=ALU.mult),
                             reads=[bkey, ("fsq", si)], writes=[("U", ui, hc, half)])
                for fo in range(16):
                    for half in range(2):
                        cols = slice(half * 512, (half + 1) * 512)
                        bank, bkey = self.ps()
                        for hc in range(4):
                            S.op("pe", lambda q, bank=bank, wi=wi, hc=hc, fo=fo, ui=ui, cols=cols: q.matmul(bank[:], W2r[wi][:, hc, fo * 128:(fo + 1) * 128], U[ui][:, hc, cols],
                                                                                                    start=(hc == 0), stop=(hc == 3)),
                                 reads=[("W2r", wi), ("U", ui, hc, half)], writes=[bkey])
                        S.op("dve", lambda q, bank=bank, fo=fo, cols=cols: q.tensor_tensor(out=self.XT[:, fo, cols], in0=bank[:], in1=self.XT[:, fo, cols], op=ALU.add),
                             reads=[bkey, ("XT", fo)], writes=[("XT", fo)])
            S.barrier()
            S.flush()

    def final_norm_store(self, do_norm):
        S = self.S
        with ExitStack() as st:
            if do_norm:
                self.rmsnorm(st, 2 * self.LW, None, "XTn", out_f32=self.XT)
            xo = self.xT_out.rearrange("(c p) l -> p c l", p=128)
            for c4 in range(4):
                rk = [("XTn", c) for c in range(c4 * 4, c4 * 4 + 4)] + [("XT", c) for c in range(c4 * 4, c4 * 4 + 4)]
                S.dma("sp", lambda q, c4=c4: q.dma_start(out=xo[:, c4 * 4:(c4 + 1) * 4, :], in_=self.XT[:, c4 * 4:(c4 + 1) * 4, :]), "st_x", reads=rk)
            S.barrier()
            S.flush()

    def exchange(self, kvkeys):
        S = self.S
        rg = [[0, 1], [2, 3], [4, 5], [6, 7]]
        for g in range(2):
            S.dma("pool", lambda q, g=g: q.collective_compute("AllGather", ALU.bypass, replica_groups=rg, ins=[self.KT_loc2[g]], outs=[self.KT_all2[g]]),
                  f"cc_k{g}", reads=kvkeys, writes=["KT_all"], inc=1)
        S.dma("pool", lambda q: q.collective_compute("AllGather", ALU.bypass, replica_groups=rg, ins=[self.V_loc2], outs=[self.V_all2]),
              "cc_v", reads=kvkeys, writes=["V_all"], inc=1)

    def build(self):
        self.declare()
        self.persistent()
        for li, layer in enumerate(self.layers):
            if self.mode in ("A", "fused"):
                self.phase_A(li)
            if self.mode in ("B", "BCD", "fused"):
                self.phase_B(li)
            if self.mode in ("BCD", "fused"):
                self.phase_C(li)
                self.phase_D(li)
        if self.mode in ("BCD", "fused"):
            self.final_norm_store(self.final_norm)
        self.S.barrier()
        self.S.flush()
        self.top.close()
        return self.nc

FUSED = True
_PROGS = {}


def _prog(mode, nlayers, final):
    key = (mode, nlayers, final)
    if key not in _PROGS:
        P = Prog(mode, list(range(nlayers)), final)
        _PROGS[key] = P.build()
    return _PROGS[key]


def _gam_arr(gs):
    return np.ascontiguousarray(np.concatenate([np.asarray(g, np.float32).reshape(16, 128).T for g in gs], axis=1))


def _own(a, j):
    return a.reshape(8, 2, 128, -1)[:, j].reshape(1024, -1)


def kernel(x, w_in, cmp_pos, cmp_w1, cmp_w2, swa_sinks, w_branch, w_out, w_mlp_in, w_mlp_out,
           norm_mix, norm_mlp, norm_final, rel_bias):
    f32 = lambda a: np.ascontiguousarray(np.asarray(a, dtype=np.float32))
    x = f32(x); w_in = f32(w_in); cmp_pos = f32(cmp_pos); cmp_w1 = f32(cmp_w1); cmp_w2 = f32(cmp_w2)
    swa_sinks = f32(swa_sinks); w_branch = f32(w_branch); w_out = f32(w_out); w_mlp_in = f32(w_mlp_in)
    w_mlp_out = f32(w_mlp_out); norm_mix = f32(norm_mix); norm_mlp = f32(norm_mlp); norm_final = f32(norm_final)
    rel_bias = f32(rel_bias)
    consts = host_consts()
    tables = [host_tables(rel_bias, j) for j in range(2)]
    posT_all = np.ascontiguousarray(np.transpose(cmp_pos, (0, 1, 3, 2)))
    cores = list(range(8))
    xT = [np.ascontiguousarray(_own(x[c // 2], c % 2).T) for c in cores]

    if FUSED:
        gam = _gam_arr([g for l in range(DEPTH) for g in (norm_mix[l], norm_mlp[l])] + [norm_final])
        sinks = np.ascontiguousarray(np.broadcast_to(swa_sinks.reshape(1, DEPTH * 8), (128, DEPTH * 8)))
        nc = _prog("fused", DEPTH, True)
        ins = []
        for c in cores:
            m = {"xT": xT[c], "w_in": w_in, "posT": posT_all, "cmp_w1": cmp_w1, "cmp_w2": cmp_w2, "w_branch": w_branch, "w_out": w_out,
                 "w_mlp_in": w_mlp_in, "w_mlp_out": w_mlp_out, "gam": gam, "sinks": sinks}
            m.update(tables[c % 2]); m.update(consts)
            ins.append(m)
        res = run_bass_kernel_spmd(nc, ins, core_ids=cores)
        xT = [res.results[c]["xT_out"] for c in cores]
    else:
        for l in range(DEPTH):
            gam = _gam_arr([norm_mix[l], norm_mlp[l], norm_final])
            sinks = np.ascontiguousarray(np.broadcast_to(swa_sinks[l][None], (128, 8)))
            base = []
            for c in cores:
                m = {"xT": xT[c], "gam": gam, "sinks": sinks}
                m.update(tables[c % 2]); m.update(consts)
                base.append(m)
            ncA = _prog("A", 1, False)
            insA = [dict(base[c], w_in=w_in[l:l + 1]) for c in cores]
            rA = run_bass_kernel_spmd(ncA, insA, core_ids=cores).results
            ncB = _prog("BCD", 1, l == DEPTH - 1)
            insB = []
            for c in cores:
                c0 = (c // 2) * 2
                m = dict(base[c], QT_d=rA[c]["QT_d"], AG_d=rA[c]["AG_d"], G_d=rA[c]["G_d"],
                         KT_all=np.stack([rA[c0]["KT_loc"], rA[c0 + 1]["KT_loc"]]), V_all=np.stack([rA[c0]["V_loc"], rA[c0 + 1]["V_loc"]]),
                         posT=posT_all[l:l + 1], cmp_w1=cmp_w1[l:l + 1], cmp_w2=cmp_w2[l:l + 1], w_branch=w_branch[l:l + 1],
                         w_out=w_out[l:l + 1], w_mlp_in=w_mlp_in[l:l + 1], w_mlp_out=w_mlp_out[l:l + 1])
                insB.append(m)
            rB = run_bass_kernel_spmd(ncB, insB, core_ids=cores).results
            xT = [rB[c]["xT_out"] for c in cores]
    out = np.zeros((4, 2048, D), np.float32)
    for c in cores:
        b, j = c // 2, c % 2
        out[b].reshape(8, 2, 128, D)[:, j] = np.asarray(xT[c], np.float32).T.reshape(8, 128, D)
    return out
```
